# Optimizing a Trainium2 kernel written in Bass

```python
import jax, jax.numpy as jnp
from jax import lax
import numpy as np

D_MODEL = 2048
BATCH = 4
SEQ = 4096
DEPTH = 1

N_MEM = 256
D_MIX = D_MODEL
D_DELTA = D_MIX // 2
D_CONV = D_MIX - D_DELTA
DN_HEADS = 8
DN_HEAD_DIM = D_DELTA // DN_HEADS
DN_CONV_W = 4
DN_CHUNK = 64
CF_CONV_W = 31
XA_HEADS = 4
XA_HEAD_DIM = D_MODEL // XA_HEADS
D_FF = 4 * D_MODEL
EPS = 1e-6
D_IN = 4 * D_DELTA + 2 * DN_HEADS + 2 * D_CONV

kernel_name = "hybrid_gdn_conformer_block"


def rms_norm(x, g):
    xf = x.astype(jnp.float32)
    y = xf * lax.rsqrt(jnp.mean(xf * xf, axis=-1, keepdims=True) + EPS)
    return (y * g.astype(jnp.float32)).astype(x.dtype)


def layer_norm(x, g, b):
    xf = x.astype(jnp.float32)
    mu = jnp.mean(xf, axis=-1, keepdims=True)
    xc = xf - mu
    var = jnp.mean(xc * xc, axis=-1, keepdims=True)
    y = xc * lax.rsqrt(var + EPS) * g.astype(jnp.float32) + b.astype(jnp.float32)
    return y.astype(x.dtype)


def l2_normalize(x):
    xf = x.astype(jnp.float32)
    return xf * lax.rsqrt(jnp.sum(xf * xf, axis=-1, keepdims=True) + EPS)


def causal_depthwise_conv(x, w):
    width, ch = w.shape
    return lax.conv_general_dilated(
        x, w[:, None, :].astype(x.dtype), window_strides=(1,), padding=[(width - 1, 0)],
        dimension_numbers=("NWC", "WIO", "NWC"), feature_group_count=ch)


def chunk_gated_delta_rule(q, k, v, g, beta):
    B, T, H, Dk = q.shape
    Dv = v.shape[-1]
    C = DN_CHUNK
    N = T // C

    def to_chunks(t):
        t = t.reshape((B, N, C, H) + t.shape[3:])
        return jnp.moveaxis(t, 3, 1)

    q, k, v, g, beta = (to_chunks(t) for t in (q, k, v, g, beta))
    q = q * (Dk ** -0.5)
    g = jnp.cumsum(g, axis=-1)
    causal = jnp.tril(jnp.ones((C, C), dtype=bool))
    strict = jnp.tril(jnp.ones((C, C), dtype=bool), -1)
    decay = jnp.exp(jnp.where(causal, g[..., :, None] - g[..., None, :], -jnp.inf))

    k_beta = k * beta[..., None]
    v_beta = v * beta[..., None]
    a = jnp.einsum("bhncd,bhnsd->bhncs", k_beta, k) * decay
    t_mat = jnp.eye(C, dtype=jnp.float32) + jnp.where(strict, a, 0.0)
    rhs = jnp.concatenate([v_beta, k_beta * jnp.exp(g)[..., None]], axis=-1)
    sol = lax.linalg.triangular_solve(t_mat, rhs, left_side=True, lower=True, unit_diagonal=True)
    u, w = sol[..., :Dv], sol[..., Dv:]

    attn = jnp.einsum("bhncd,bhnsd->bhncs", q, k) * decay
    q_dec = q * jnp.exp(g)[..., None]
    g_last = g[..., -1]
    k_dec = k * jnp.exp(g_last[..., None] - g)[..., None]

    def step(S, xs):
        q_c, k_c, u_c, w_c, attn_c, gl = xs
        v_new = u_c - jnp.einsum("bhcd,bhde->bhce", w_c, S)
        o = jnp.einsum("bhcd,bhde->bhce", q_c, S) + jnp.einsum("bhcs,bhse->bhce", attn_c, v_new)
        S = S * jnp.exp(gl)[..., None, None] + jnp.einsum("bhcd,bhce->bhde", k_c, v_new)
        return S, o

    xs = tuple(jnp.moveaxis(t, 2, 0) for t in (q_dec, k_dec, u, w, attn, g_last))
    S0 = jnp.zeros((B, H, Dk, Dv), dtype=jnp.float32)
    _, o = lax.scan(step, S0, xs)
    return jnp.transpose(o, (1, 0, 3, 2, 4)).reshape(B, T, H, Dv)


def hybrid_mixer(xn, w_in, dn_conv_w, dn_a_log, dn_dt_bias, dn_norm_g,
                 cf_dw_w, cf_dw_b, cf_ln_g, cf_ln_b, w_out):
    B, T, _ = xn.shape
    proj = xn @ w_in
    qkv, z, b_raw, a_raw, glu = jnp.split(
        proj, [3 * D_DELTA, 4 * D_DELTA, 4 * D_DELTA + DN_HEADS, 4 * D_DELTA + 2 * DN_HEADS], axis=-1)

    qkv = jax.nn.silu(causal_depthwise_conv(qkv, dn_conv_w))
    q, k, v = jnp.split(qkv, 3, axis=-1)
    hd = (B, T, DN_HEADS, DN_HEAD_DIM)
    q = l2_normalize(q.reshape(hd))
    k = l2_normalize(k.reshape(hd))
    v = v.reshape(hd).astype(jnp.float32)
    beta = jax.nn.sigmoid(b_raw.astype(jnp.float32))
    g = -jnp.exp(dn_a_log.astype(jnp.float32)) * jax.nn.softplus(
        a_raw.astype(jnp.float32) + dn_dt_bias.astype(jnp.float32))
    o = chunk_gated_delta_rule(q, k, v, g, beta)
    o = rms_norm(o, dn_norm_g) * jax.nn.silu(z.reshape(hd).astype(jnp.float32))
    o = o.reshape(B, T, D_DELTA).astype(xn.dtype)

    c = glu[..., :D_CONV] * jax.nn.sigmoid(glu[..., D_CONV:])
    c = causal_depthwise_conv(c, cf_dw_w) + cf_dw_b
    c = jax.nn.silu(layer_norm(c, cf_ln_g, cf_ln_b))

    return jnp.concatenate([o, c], axis=-1) @ w_out


def memory_cross_attention(hn, mem_n, w_q, w_k, w_v, w_o):
    B, T, _ = hn.shape
    M = mem_n.shape[1]
    q = (hn @ w_q).reshape(B, T, XA_HEADS, XA_HEAD_DIM)
    k = (mem_n @ w_k).reshape(B, M, XA_HEADS, XA_HEAD_DIM)
    v = (mem_n @ w_v).reshape(B, M, XA_HEADS, XA_HEAD_DIM)
    s = jnp.einsum("bthd,bmhd->bhtm", q, k).astype(jnp.float32) * (XA_HEAD_DIM ** -0.5)
    p = jax.nn.softmax(s, axis=-1).astype(v.dtype)
    o = jnp.einsum("bhtm,bmhd->bthd", p, v).reshape(B, T, D_MODEL)
    return o @ w_o


def squared_relu_mlp(hn, w1, w2):
    return jnp.square(jax.nn.relu(hn @ w1)) @ w2


def setup_inputs(seed: int = 0) -> dict:
    key = jax.random.key(seed)
    ks = jax.random.split(key, 24)
    f32 = jnp.float32
    L = DEPTH

    def dense(k, fan_in, fan_out):
        return jax.random.normal(k, (L, fan_in, fan_out), f32) * fan_in ** -0.5

    def gain(k, n):
        return 1.0 + 0.02 * jax.random.normal(k, (L, n), f32)

    dt = jnp.exp(jax.random.uniform(ks[6], (L, DN_HEADS), f32, np.log(1e-3), np.log(1e-1)))
    return {
        "x": jax.random.normal(ks[0], (BATCH, SEQ, D_MODEL), f32),
        "mem": jax.random.normal(ks[1], (BATCH, N_MEM, D_MODEL), f32),
        "norm_mix_g": gain(ks[2], D_MODEL),
        "w_in": dense(ks[3], D_MODEL, D_IN),
        "dn_conv_w": jax.random.normal(ks[4], (L, DN_CONV_W, 3 * D_DELTA), f32) * DN_CONV_W ** -0.5,
        "dn_a_log": jnp.log(jax.random.uniform(ks[5], (L, DN_HEADS), f32, 1.0, 16.0)),
        "dn_dt_bias": dt + jnp.log(-jnp.expm1(-dt)),
        "dn_norm_g": gain(ks[7], DN_HEAD_DIM),
        "cf_dw_w": jax.random.normal(ks[8], (L, CF_CONV_W, D_CONV), f32) * CF_CONV_W ** -0.5,
        "cf_dw_b": 0.02 * jax.random.normal(ks[9], (L, D_CONV), f32),
        "cf_ln_g": gain(ks[10], D_CONV),
        "cf_ln_b": 0.02 * jax.random.normal(ks[11], (L, D_CONV), f32),
        "w_out": dense(ks[12], D_MIX, D_MODEL),
        "norm_xa_g": gain(ks[13], D_MODEL),
        "norm_mem_g": gain(ks[14], D_MODEL),
        "xa_wq": dense(ks[15], D_MODEL, D_MODEL),
        "xa_wk": dense(ks[16], D_MODEL, D_MODEL),
        "xa_wv": dense(ks[17], D_MODEL, D_MODEL),
        "xa_wo": dense(ks[18], D_MODEL, D_MODEL),
        "norm_mlp_g": gain(ks[19], D_MODEL),
        "mlp_w1": dense(ks[20], D_MODEL, D_FF),
        "mlp_w2": dense(ks[21], D_FF, D_MODEL),
        "norm_final_g": 1.0 + 0.02 * jax.random.normal(ks[22], (D_MODEL,), f32),
    }


def reference(x, mem, norm_mix_g, w_in, dn_conv_w, dn_a_log, dn_dt_bias, dn_norm_g,
              cf_dw_w, cf_dw_b, cf_ln_g, cf_ln_b, w_out, norm_xa_g, norm_mem_g,
              xa_wq, xa_wk, xa_wv, xa_wo, norm_mlp_g, mlp_w1, mlp_w2, norm_final_g):
    h = x
    for i in range(DEPTH):
        h = h + hybrid_mixer(rms_norm(h, norm_mix_g[i]), w_in[i], dn_conv_w[i], dn_a_log[i],
                             dn_dt_bias[i], dn_norm_g[i], cf_dw_w[i], cf_dw_b[i],
                             cf_ln_g[i], cf_ln_b[i], w_out[i])
        h = h + memory_cross_attention(rms_norm(h, norm_xa_g[i]), rms_norm(mem, norm_mem_g[i]),
                                       xa_wq[i], xa_wk[i], xa_wv[i], xa_wo[i])
        h = h + squared_relu_mlp(rms_norm(h, norm_mlp_g[i]), mlp_w1[i], mlp_w2[i])
    return rms_norm(h, norm_final_g)
```

```python
import numpy as np
import concourse.bass as bass
import concourse.mybir as mybir
from concourse.bass_utils import run_bass_kernel_spmd

F32 = mybir.dt.float32
BF16 = mybir.dt.bfloat16
AF = mybir.ActivationFunctionType
ALU = mybir.AluOpType
AX = mybir.AxisListType

D = 2048
TB = 512
EPS = 1e-6
NEG = -30000.0
O_ID, O_U, O_ONES, O_NEGONES, O_STRICT, O_NEGM = 0, 128, 256, 384, 512, 640
O_GMIX, O_GXA, O_GMLP, O_GMEM = 768, 784, 800, 816
O_DNW, O_CFW, O_CFB, O_LNG, O_LNB = 832, 928, 1176, 1184, 1192
O_ALOG, O_DTB, O_GN, O_EPS, O_ONE = 1200, 1208, 1216, 1344, 1345
PW = 1348
MMLOG = []
import os as _os
NCHAIN = int(_os.environ.get("NCHAIN", "2"))


class _Rec:
    def __getattr__(self, name):
        def f(*a, **k):
            self.call = (name, a, k)
            return self
        return f


class Sched:
    ENG = ("pe", "act", "dve", "pool", "sp")

    def __init__(self, nc):
        self.nc = nc
        self.streams = {e: [] for e in self.ENG}
        self.count = {e: 0 for e in self.ENG}
        self.waited = {}
        self.state = {}
        self.dma_cnt = {}
        self.semnames = set("e_" + e for e in self.ENG)

    def _deps(self, eng, reads, writes, join_sem=None):
        deps = []
        for k in reads:
            st = self.state.get(k)
            if st and st[0]:
                deps.append(st[0])
        for k in writes:
            st = self.state.get(k)
            if st:
                if st[0] and not (join_sem and st[0][0] == join_sem):
                    deps.append(st[0])
                for sn, (v, e) in st[1].items():
                    deps.append((sn, v, e))
        for (sn, v, src) in deps:
            if src == eng and eng == "pe":
                continue
            if self.waited.get((eng, sn), 0) >= v:
                continue
            self.waited[(eng, sn)] = v
            self.streams[eng].append(("wait", sn, v))

    def _commit(self, tok, reads, writes):
        for k in reads:
            st = self.state.setdefault(k, [None, {}])
            old = st[1].get(tok[0])
            if old is None or old[0] < tok[1]:
                st[1][tok[0]] = (tok[1], tok[2])
        for k in writes:
            self.state[k] = [tok, {}]

    def op(self, eng, fn, reads=(), writes=(), signal=True):
        self._deps(eng, reads, writes)
        if signal:
            self.count[eng] += 1
            tok = ("e_" + eng, self.count[eng], eng)
        else:
            tok = ("e_" + eng, self.count[eng] + 1, eng)
        rec = _Rec()
        fn(rec)
        self.streams[eng].append(("op", rec.call, ("e_" + eng, 1) if signal else None))
        self._commit(tok, reads, writes)

    def dma(self, q, fn, dsem, reads=(), writes=(), join=False):
        self.semnames.add(dsem)
        self._deps(q, reads, writes, join_sem=dsem if join else None)
        self.dma_cnt[dsem] = self.dma_cnt.get(dsem, 0) + 16
        tok = (dsem, self.dma_cnt[dsem], "dma")
        rec = _Rec()
        fn(rec)
        self.streams[q].append(("op", rec.call, (dsem, 16)))
        self._commit(tok, reads, writes)

    def fence(self, new_keys, old_keys):
        merged = {}
        for k in old_keys:
            st = self.state.get(k)
            if not st:
                continue
            toks = list(st[1].items())
            if st[0]:
                toks.append((st[0][0], (st[0][1], st[0][2])))
            for sn, (v, e) in toks:
                if sn not in merged or merged[sn][0] < v:
                    merged[sn] = (v, e)
        for k in new_keys:
            st = self.state.setdefault(k, [None, {}])
            for sn, ve in merged.items():
                if sn not in st[1] or st[1][sn][0] < ve[0]:
                    st[1][sn] = ve

    def final_waits(self, eng, keys):
        self._deps(eng, keys, ())

    def emit(self):
        nc = self.nc
        sems = {n: nc.alloc_semaphore(n) for n in sorted(self.semnames)}
        handles = {"pe": "tensor", "act": "scalar", "dve": "vector", "pool": "gpsimd", "sp": "sync"}
        with nc.Block() as block:
            for e in self.ENG:
                items = self.streams[e]

                def body(engh, items=items):
                    for it in items:
                        if it[0] == "wait":
                            engh.wait_ge(sems[it[1]], it[2])
                        else:
                            ins = getattr(engh, it[1][0])(*it[1][1], **it[1][2])
                            if it[2] is not None:
                                ins.then_inc(sems[it[2][0]], it[2][1])

                getattr(block, handles[e])(body)


def build(NPRE, NOWN, taps=()):
    nc = bass.Bass("TRN2", target_bir_lowering=False)
    S = Sched(nc)
    NBLK = NPRE + NOWN

    def din(name, shape):
        return nc.dram_tensor(name, list(shape), F32, kind="ExternalInput").ap()

    xin = din("xin", [NBLK * TB, D])
    memd = din("mem", [256, D])
    prmd = din("prm", [128, PW])
    gfrd = din("gfr", [128, D])
    w_in = din("w_in", [D, 6160])
    w_out = din("w_out", [D, D])
    wq = din("xa_wq", [D, D])
    wk = din("xa_wk", [D, D])
    wv = din("xa_wv", [D, D])
    wo = din("xa_wo", [D, D])
    w1 = din("mlp_w1", [D, 8192])
    w2 = din("mlp_w2", [8192, D])
    outd = nc.dram_tensor("out", [NOWN * TB, D], F32, kind="ExternalOutput").ap()
    tapd = {}

    def sb(name, shape, dt):
        return nc.alloc_sbuf_tensor("s_" + name, list(shape), dt)

    prm = sb("prm", [128, PW], F32)
    idb = sb("idb", [128, 128], BF16)
    onesb = sb("onesb", [128, 128], BF16)
    negmb = sb("negmb", [128, 4, 128], BF16)
    negA = sb("negA", [128, 8], F32)
    h_tok = sb("h_tok", [128, 4, D], F32)
    hnT2 = sb("hnT", [128, 16 * TB], BF16)
    hnT = hnT2[:, :].rearrange("p (c t) -> p c t", c=16)
    NW = 2
    wb = [sb(f"wb{i}", [128, 16, 512], BF16) for i in range(NW)]
    KT = sb("KT", [128, 16, 256], BF16)
    Vm = sb("Vm", [128, 2, D], BF16)
    Sst = sb("Sst", [128, 8, 128], F32)
    Sbf = sb("Sbf", [128, 8, 128], BF16)
    hq = sb("hq", [128, 24, 3], BF16)
    hc = sb("hc", [128, 8, 30], BF16)
    ba = sb("ba", [128, 4, 16], F32)
    sm = sb("sm", [128, 64], F32)
    rawc = [sb(f"rawc{i}", [128, 515], BF16) for i in range(2)]
    cgl = [sb(f"cgl{i}", [128, 542], BF16) for i in range(2)]
    dgs = [sb(f"dg{i}", [128, 128], BF16) for i in range(12)]
    RA = sb("RA", [128, 20480], BF16)
    RAf = RA.bitcast(F32)
    RB = sb("RB", [128, 16384], BF16)
    RBf = RB.bitcast(F32)

    qkvT = RA[:, 0:6144].rearrange("p (c t) -> p c t", c=12)
    cx = RAf[:, 0:4096].rearrange("p (c t) -> p c t", c=8)
    zs = RA[:, 8192:12288].rearrange("p (t n) -> p t n", t=4)
    catT = RA[:, 12288:20480].rearrange("p (c t) -> p c t", c=16)
    xqT = RA[:, 0:8192].rearrange("p (c t) -> p c t", c=16)
    pT = RA[:, 8192:12288].rearrange("p (h m t) -> p h m t", h=4, m=2)
    xoT = RA[:, 12288:20480].rearrange("p (c t) -> p c t", c=16)
    gfr = RAf[:, 0:2048]
    A_MIX = ["qkvT", "cx", "zs", "catT", "cxb", "cxq", "lnm", "lnr"]
    A_XA = ["xqT", "pT", "xoT"]
    A_MLP = ["gfr"]
    a1T = RB[:, 0:16384].rearrange("p (c t) -> p c t", c=32)
    hs = sb("hs", [128, 2048], BF16)[:, :]
    hsb = [hs, sb("hs1", [128, 2048], BF16)[:, :]]
    ptmp = RBf[:, 0:1024].rearrange("p (h m) -> p h m", h=4)
    pn = RB[:, 2048:3072].rearrange("p (h m) -> p h m", h=4)
    rtmp = sb("rtmp", [128, 512], BF16)[:, :]
    tmpa = sb("tmpa", [128, 512], BF16)[:, :]
    tmps = sb("tmps", [128, 512], BF16)[:, :]
    tmpa2 = sb("tmpa2", [128, 512], BF16)[:, :]
    tmps2 = sb("tmps2", [128, 512], BF16)[:, :]

    def v3(ap2, h=4):
        return ap2.rearrange("p (h e) -> p h e", h=h)

    o = 0

    def bvb():
        nonlocal o
        r = RB[:, o:o + 512]
        o += 512
        return r

    def bvf():
        nonlocal o
        r = RBf[:, o // 2:o // 2 + 512]
        o += 1024
        return r
    Dincl = bvf(); Dstr = bvf(); Gb = bvf(); GU = bvf()
    Nb = [bvb(), bvb()]; Mb = [bvb(), bvb()]
    Rf = bvf(); Rb = bvb()
    kntok = bvb(); Bk = bvb(); kdec = bvb(); bvv = bvb()
    knT = bvb(); attn = bvb(); attnT = bvb(); negwT = bvb(); vnew = bvb()
    yb = bvf(); tb_ = bvf(); og = bvb()
    sqy = tb_
    sqt = RB[:, o:o + 1024]; o += 1024
    cxb = RA[:, 8192:8704]; cxq = RA[:, 8704:9216]; lnm = RAf[:, 4608:5120]; lnr = RAf[:, 5120:5632]
    assert o <= 16384, o
    B_DN0 = ["Dincl", "Dstr", "Gb", "GU", "Nb0", "Nb1", "Mb0", "Mb1", "Rf", "Rb", "kntok", "Bk", "kdec", "bvv",
             "knT", "attn", "attnT", "negwT", "vnew", "yb", "tb", "og"]
    B_DN = [k + s for k in B_DN0 for s in ("A", "B", "C", "D")] + ["sqt"]
    B_XA = ["ptmp", "pn"]
    B_MLP = ["a1T"]
    scn = ["beta", "graw", "xs", "axs", "es", "ls", "rq", "rk", "gcs", "gam", "eL", "dl", "edec", "sB", "sKd",
           "nbeta", "sP1", "ssy", "ro", "tq"]
    sc = {n: sb("sc_" + n, [128, 4, 4], F32) for n in scn}
    rqk = sb("rqk", [128, 4, 8], F32)
    xsm = sb("xsm", [128, 16], F32)

    ps = [nc.alloc_psum_tensor(f"ps{i}", [128, 512], F32) for i in range(8)]
    psb = [p.bitcast(BF16) for p in ps]
    gctr = [0]
    dctr = [0]

    def gbank():
        i = gctr[0] % 4
        gctr[0] += 1
        return i

    dfree_set = {0, 1, 2, 3, 4, 5, 6, 7}

    def dbank():
        for k in range(8):
            i = (4 + dctr[0] + k) % 8
            if i in dfree_set:
                dfree_set.discard(i)
                dctr[0] = (i - 4) % 8 + 1
                return i
        raise RuntimeError("no free DN psum bank")

    def dfree(*bs_):
        for i in bs_:
            assert i not in dfree_set
            dfree_set.add(i)

    P = lambda off, n=128: prm[:, off:off + n]

    def act(fn, reads, writes):
        S.op("act", fn, reads=reads, writes=writes)

    def dve(fn, reads, writes):
        S.op("dve", fn, reads=reads, writes=writes)

    def mm(out, lhsT, rhs, start, stop, reads, writes, signal):
        MMLOG.append((writes[0], bool(start), bool(stop), str(out)[:80]))
        S.op("pe", lambda e: e.matmul(out, lhsT=lhsT, rhs=rhs, start=start, stop=stop),
             reads=reads, writes=writes, signal=signal)

    def tr(out, in_, reads, writes, signal):
        MMLOG.append((writes[0], "T", "T", ""))
        S.op("pe", lambda e: e.transpose(out=out, in_=in_, identity=idb[:]),
             reads=list(reads) + ["idb"], writes=writes, signal=signal)

    def rsqrt_small(dst, src, scale, rk, wk, tmp):
        act(lambda e: e.activation(out=tmp, in_=src, func=AF.Ln, scale=scale, bias=prm[:, O_EPS:O_EPS + 1]),
            list(rk) + ["prm"], ["rs_tmp"])
        act(lambda e: e.activation(out=dst, in_=tmp, func=AF.Exp, scale=-0.5), ["rs_tmp"], wk)

    wctr = [0]

    def load_w(parts):
        i = wctr[0] % NW
        wctr[0] += 1
        for (src, co) in parts:
            n = src.shape[1]
            v = src.rearrange("(kc p) n -> p kc n", p=128)
            for q4 in range(4):
                S.dma("pool", lambda e, i=i, v=v, q4=q4, co=co, n=n: e.dma_start(
                    out=wb[i][:, 4 * q4:4 * q4 + 4, co:co + n], in_=v[:, 4 * q4:4 * q4 + 4, :]),
                    f"d_w{i}", writes=[f"w{i}"], join=True)
        return i

    steps = []
    tapkeys = []

    def tap(name, ap, keys):
        if name not in taps:
            return
        d = nc.dram_tensor("tap_" + name, list(ap.shape), ap.dtype, kind="ExternalOutput").ap()
        S.dma("sp", lambda e: e.dma_start(out=d, in_=ap), "d_tap_" + name, reads=keys, writes=["tap_" + name])
        tapkeys.append("tap_" + name)

    def tapstep(name, ap, keys):
        steps.append((None, lambda _: tap(name, ap, keys)))

    def run_steps():
        widx = [i for i, s in enumerate(steps) if s[0] is not None]
        slot = {}
        nxt = [0]

        def ensure(n_ahead_of):
            while nxt[0] < len(widx) and nxt[0] <= n_ahead_of:
                slot[widx[nxt[0]]] = load_w(steps[widx[nxt[0]]][0])
                nxt[0] += 1

        pos = 0
        ensure(0)
        for i, (parts, fn) in enumerate(steps):
            if parts is not None:
                ensure(pos + NW - 1)
                fn(slot[i])
                pos += 1
            else:
                fn(None)

    S.dma("sp", lambda e: e.dma_start(out=prm[:], in_=prmd), "d_prm", writes=["prm"])
    act(lambda e: e.copy(out=idb[:], in_=P(O_ID)), ["prm"], ["idb"])
    act(lambda e: e.copy(out=onesb[:], in_=P(O_ONES)), ["prm"], ["onesb"])
    for h in range(4):
        act(lambda e, h=h: e.copy(out=negmb[:, h, :], in_=P(O_NEGM)), ["prm"], ["negmb"])
    act(lambda e: e.activation(out=negA[:], in_=P(O_ALOG, 8), func=AF.Exp), ["prm"], ["negA"])
    dve(lambda e: e.tensor_scalar(out=negA[:], in0=negA[:], scalar1=-1.0, scalar2=None, op0=ALU.mult), ["negA"], ["negA"])
    dve(lambda e: e.memset(Sst[:], 0.0), [], [f"Sst{i}{s}" for i in range(2) for s in "ABCD"])
    dve(lambda e: e.memset(Sbf[:], 0.0), [], [f"Sbf{i}{s}" for i in range(2) for s in "ABCD"])
    dve(lambda e: e.memset(hq[:], 0.0), [], ["hq"])
    dve(lambda e: e.memset(hc[:], 0.0), [], ["hc"])
    dve(lambda e: e.memset(RB[:, 0:16384], 0.0), [], B_DN)
    dve(lambda e: e.memset(RA[:, :], 0.0), [], A_MIX + A_XA + A_MLP)

    def norm_T(src_fn, src_keys, gcol_off, dst, dst_key, ntile, tok_w):
        def part1(t):
            src = src_fn(t)
            hb, hk = hsb[t % 2], f"hs{t % 2}"
            act(lambda e: e.activation(out=hb, in_=src, func=AF.Square, accum_out=sm[:, 0:1]),
                src_keys(t), [hk, "sm0"])
            rsqrt_small(sm[:, 1:2], sm[:, 0:1], 1.0 / D, ["sm0"], ["sm1"], sm[:, 2:3])
            act(lambda e: e.activation(out=hb, in_=src, func=AF.Copy, scale=sm[:, 1:2]),
                src_keys(t) + ["sm1"], [hk])

        def part2(t):
            hb, hk = hsb[t % 2], f"hs{t % 2}"
            for half in range(2):
                b = dbank()
                pv = psb[b][:, 0:1024].rearrange("p (c t) -> p c t", c=8)
                for c in range(8):
                    kc = half * 8 + c
                    tr(pv[:, c, :], hb[:, kc * 128:(kc + 1) * 128], [hk], [f"ps{b}"], c == 7)
                dve(lambda e: e.tensor_tensor(
                    out=dst[:, half * 8:half * 8 + 8, t * 128:(t + 1) * 128], in0=pv,
                    in1=prm[:, gcol_off + half * 8:gcol_off + half * 8 + 8].unsqueeze(2).to_broadcast([128, 8, 128]),
                    op=ALU.mult), [f"ps{b}", "prm"], [dst_key])
                dfree(b)

        for t in range(ntile):
            part1(t)
            if t > 0:
                part2(t - 1)
        part2(ntile - 1)

    def gemm_fm(wi, wcol, rhs_fn, n, nk, rk, evac):
        b = gbank()
        for kc in range(nk):
            mm(ps[b][:, 0:n], wb[wi][:, kc, wcol:wcol + 128], rhs_fn(kc), kc == 0, kc == nk - 1,
               [f"w{wi}"] + rk, [f"ps{b}"], kc == nk - 1)
        evac(b)

    def gemm_tm(wi, ncols, lhs_fn, nk, rk, evac, b=None, first=True, last=True):
        if b is None:
            b = gbank()
        for kc in range(nk):
            mm(ps[b][:, 0:ncols], lhs_fn(kc), wb[wi][:, kc, 0:ncols], first and kc == 0, last and kc == nk - 1,
               [f"w{wi}"] + rk, [f"ps{b}"], kc == nk - 1)
        if last:
            evac(b)
        return b

    memt = [h_tok[:, 0, :], h_tok[:, 1, :]]

    def mem_prep(_):
        for t in range(2):
            S.dma("sp", lambda e, t=t: e.dma_start(out=memt[t], in_=memd[t * 128:(t + 1) * 128, :]),
                  f"d_x{t}", writes=[f"h{t}"])
        norm_T(lambda t: memt[t], lambda t: [f"h{t}"], O_GMEM, hnT, "hnT", 2, 256)
    steps.append((None, mem_prep))
    for g in range(4):
        def kstep(wi, g=g):
            for j in range(4):
                n = g * 4 + j
                gemm_fm(wi, j * 128, lambda kc: hnT[:, kc, 0:256], 256, 16, ["hnT"],
                        lambda b, n=n: act(lambda e: e.copy(out=KT[:, n, :], in_=ps[b][:, 0:256]), [f"ps{b}"], ["KT"]))
        steps.append(([(wk[:, g * 512:(g + 1) * 512], 0)], kstep))
    for g in range(4):
        def vstep(wi, g=g):
            for t in range(2):
                gemm_tm(wi, 512, lambda kc, t=t: hnT[:, kc, t * 128:(t + 1) * 128], 16, ["hnT"],
                        lambda b, t=t: act(lambda e: e.copy(out=Vm[:, t, g * 512:(g + 1) * 512], in_=ps[b][:]),
                                           [f"ps{b}"], ["Vm"]))
        steps.append(([(wv[:, g * 512:(g + 1) * 512], 0)], vstep))

    def block(blk):
        own = blk >= NPRE
        need_qg = own or blk == NPRE - 1
        ob = blk - NPRE

        def load_x(_):
            S.fence(A_MIX, A_MLP + A_XA)
            S.fence(B_DN + ["hs"], B_MLP + B_XA)
            for t in range(4):
                S.dma("sp", lambda e, t=t: e.dma_start(out=h_tok[:, t, :],
                                                       in_=xin[blk * TB + t * 128: blk * TB + (t + 1) * 128, :]),
                      f"d_x{t}", writes=[f"h{t}"])
            norm_T(lambda t: h_tok[:, t, :], lambda t: [f"h{t}"], O_GMIX, hnT, "hnT", 4, TB)
        steps.append((None, load_x))
        tapstep(f"hnT{blk}", hnT2[:, :], ["hnT"])

        if own:
            for g in range(2):
                def zstep(wi, g=g):
                    for t in range(4):
                        def ev(b, t=t):
                            act(lambda e: e.activation(out=hs[:, 0:512], in_=ps[b][:], func=AF.Silu), [f"ps{b}"], ["hs0"])
                            dve(lambda e: e.tensor_tensor(
                                out=zs[:, t, g * 512:(g + 1) * 512].rearrange("p (h e) -> p h e", h=4),
                                in0=hs[:, 0:512].rearrange("p (h e) -> p h e", h=4),
                                in1=prm[:, O_GN:O_GN + 128].unsqueeze(1).to_broadcast([128, 4, 128]), op=ALU.mult),
                                ["hs0", "prm"], ["zs"])
                        gemm_tm(wi, 512, lambda kc, t=t: hnT[:, kc, t * 128:(t + 1) * 128], 16, ["hnT"], ev)
                steps.append(([(w_in[:, 3072 + g * 512:3072 + (g + 1) * 512], 0)], zstep))

        def bastep(wi):
            for t in range(4):
                gemm_tm(wi, 16, lambda kc, t=t: hnT[:, kc, t * 128:(t + 1) * 128], 16, ["hnT"],
                        lambda b, t=t: act(lambda e: e.copy(out=ba[:, t, :], in_=ps[b][:, 0:16]), [f"ps{b}"], ["ba"]))
        steps.append(([(w_in[:, 4096:4112], 0)], bastep))
        tapstep(f"ba{blk}", ba[:, :, :], ["ba"])
        tapstep(f"zs{blk}", RA[:, 8192:12288], ["zs"])

        for hg in range(2):
            for part in range(3):
                if part == 0 and not need_qg:
                    continue

                def qkvstep(wi, part=part, hg=hg):
                    def conv(j):
                        c24 = part * 8 + hg * 4 + j
                        cl = part * 4 + j
                        rc = rawc[c24 % 2]
                        rkey = f"rawc{c24 % 2}"
                        act(lambda e: e.copy(out=hq[:, c24, :], in_=rc[:, 512:515]), [rkey], ["hq"])
                        b2 = gbank()
                        for tp in range(4):
                            dg = dgs[(c24 * 4 + tp) % 12]
                            dk = f"dg{(c24 * 4 + tp) % 12}"
                            act(lambda e: e.activation(
                                out=dg[:], in_=P(O_ID), func=AF.Copy, scale=prm[:, O_DNW + c24 * 4 + tp:O_DNW + c24 * 4 + tp + 1]),
                                ["prm"], [dk])
                            mm(ps[b2][:], dg[:], rc[:, tp:tp + 512], tp == 0, tp == 3, [dk, rkey], [f"ps{b2}"], tp == 3)
                        act(lambda e: e.activation(out=qkvT[:, cl, :], in_=ps[b2][:], func=AF.Silu), [f"ps{b2}"], ["qkvT"])

                    if part == 0 and not own:
                        for j in range(4):
                            c24 = part * 8 + hg * 4 + j
                            rc = rawc[c24 % 2]
                            rkey = f"rawc{c24 % 2}"
                            gemm_fm(wi, j * 128, lambda kc: hnT[:, kc, 384:512], 128, 16, ["hnT"],
                                    lambda b: act(lambda e: e.copy(out=rc[:, 387:515], in_=ps[b][:, 0:128]), [f"ps{b}"], [rkey]))
                            act(lambda e: e.copy(out=hq[:, c24, :], in_=rc[:, 512:515]), [rkey], ["hq"])
                        return
                    for j in range(4):
                        c24 = part * 8 + hg * 4 + j
                        rc = rawc[c24 % 2]
                        rkey = f"rawc{c24 % 2}"
                        act(lambda e: e.copy(out=rc[:, 0:3], in_=hq[:, c24, :]), ["hq"], [rkey])
                        gemm_fm(wi, j * 128, lambda kc: hnT[:, kc, :], 512, 16, ["hnT"],
                                lambda b: act(lambda e: e.copy(out=rc[:, 3:515], in_=ps[b][:]), [f"ps{b}"], [rkey]))
                        if j > 0:
                            conv(j - 1)
                    conv(3)
                col = part * 1024 + hg * 512
                steps.append(([(w_in[:, col:col + 512], 0)], qkvstep))
            tapstep(f"qkvT{blk}_{hg}", RA[:, 0:6144], ["qkvT"])
            steps.append((None, lambda _, hg=hg: dn_group(hg, own, ob)))
            tapstep(f"S{blk}_{hg}", Sst[:, :, :], [f"Sst{i}{s}" for i in range(2) for s in "ABCD"])
            tapstep(f"cat{blk}_{hg}", RA[:, 12288:20480], ["catT"])

        if need_qg:
            for g in range(4):
                def glustep(wi, g=g):
                    if g == 0 and own:
                        S.fence(["cx"], ["qkvT"])
                        S.fence(["cxb", "cxq", "lnm", "lnr"], ["zs"])
                    if not own:
                        for j in range(2):
                            c = g * 2 + j
                            cg = cgl[c % 2]
                            ckey = f"cgl{c % 2}"
                            gemm_fm(wi, j * 128, lambda kc: hnT[:, kc, 384:512], 128, 16, ["hnT"],
                                    lambda b: act(lambda e: e.copy(out=tmpa[:, 0:128], in_=ps[b][:, 0:128]), [f"ps{b}"], ["tmpa"]))
                            gemm_fm(wi, 256 + j * 128, lambda kc: hnT[:, kc, 384:512], 128, 16, ["hnT"],
                                    lambda b: act(lambda e: e.activation(out=tmps[:, 0:128], in_=ps[b][:, 0:128], func=AF.Sigmoid),
                                                  [f"ps{b}"], ["tmps"]))
                            dve(lambda e: e.tensor_tensor(out=cg[:, 414:542], in0=tmpa[:, 0:128], in1=tmps[:, 0:128], op=ALU.mult),
                                ["tmpa", "tmps"], [ckey])
                            conf_chunk(c, cg, ckey, own)
                        return
                    for j in range(2):
                        c = g * 2 + j
                        cg = cgl[c % 2]
                        ckey = f"cgl{c % 2}"
                        ta, ts_ = (tmpa, tmps) if j == 0 else (tmpa2, tmps2)
                        ka, ks = ("tmpa", "tmps") if j == 0 else ("tmpa2", "tmps2")
                        gemm_fm(wi, j * 128, lambda kc: hnT[:, kc, :], 512, 16, ["hnT"],
                                lambda b: act(lambda e: e.copy(out=ta, in_=ps[b][:]), [f"ps{b}"], [ka]))
                        gemm_fm(wi, 256 + j * 128, lambda kc: hnT[:, kc, :], 512, 16, ["hnT"],
                                lambda b: act(lambda e: e.activation(out=ts_, in_=ps[b][:], func=AF.Sigmoid),
                                              [f"ps{b}"], [ks]))
                        act(lambda e: e.copy(out=cg[:, 0:30], in_=hc[:, c, :]), ["hc"], [ckey])
                        dve(lambda e: e.tensor_tensor(out=cg[:, 30:542], in0=ta, in1=ts_, op=ALU.mult), [ka, ks], [ckey])
                    for j in range(2):
                        c = g * 2 + j
                        conf_chunk(c, cgl[c % 2], f"cgl{c % 2}", own)
                steps.append(([(w_in[:, 4112 + g * 256:4112 + (g + 1) * 256], 0),
                               (w_in[:, 5136 + g * 256:5136 + (g + 1) * 256], 256)], glustep))
        if not own:
            return
        steps.append((None, lambda _: conf_ln()))
        tapstep(f"catc{blk}", RA[:, 12288:20480], ["catT"])

        def resid_step(src, skeys):
            def f(wi, g):
                for t in range(4):
                    gemm_tm(wi, 512, lambda kc, t=t: src[:, kc, t * 128:(t + 1) * 128], 16, skeys,
                            lambda b, t=t: dve(lambda e: e.tensor_tensor(
                                out=h_tok[:, t, g * 512:(g + 1) * 512], in0=ps[b][:], in1=h_tok[:, t, g * 512:(g + 1) * 512],
                                op=ALU.add), [f"ps{b}", f"h{t}"], [f"h{t}"]))
            return f
        for g in range(4):
            steps.append(([(w_out[:, g * 512:(g + 1) * 512], 0)], lambda wi, g=g: resid_step(catT, ["catT"])(wi, g)))

        tapstep(f"h1_{blk}", h_tok[:, :, :], ["h0", "h1", "h2", "h3"])
        def xa_norm(_):
            S.fence(A_XA, A_MIX)
            S.fence(B_XA, B_DN)
            norm_T(lambda t: h_tok[:, t, :], lambda t: [f"h{t}"], O_GXA, hnT, "hnT", 4, TB)
        steps.append((None, xa_norm))
        for g in range(4):
            def qstep(wi, g=g):
                for j in range(4):
                    n = g * 4 + j
                    gemm_fm(wi, j * 128, lambda kc: hnT[:, kc, :], 512, 16, ["hnT"],
                            lambda b, n=n: act(lambda e: e.copy(out=xqT[:, n, :], in_=ps[b][:]), [f"ps{b}"], ["xqT"]))
            steps.append(([(wq[:, g * 512:(g + 1) * 512], 0)], qstep))
        steps.append((None, lambda _: xa_core()))
        for g in range(4):
            steps.append(([(wo[:, g * 512:(g + 1) * 512], 0)], lambda wi, g=g: resid_step(xoT, ["xoT"])(wi, g)))

        tapstep(f"h2_{blk}", h_tok[:, :, :], ["h0", "h1", "h2", "h3"])
        def mlp_norm(_):
            S.fence(A_MLP, A_XA)
            S.fence(B_MLP, B_XA + B_DN + ["hs"])
            norm_T(lambda t: h_tok[:, t, :], lambda t: [f"h{t}"], O_GMLP, hnT, "hnT", 4, TB)
            S.dma("sp", lambda e: e.dma_start(out=gfr, in_=gfrd), "d_gfr", writes=["gfr"])
        steps.append((None, mlp_norm))
        for half in range(2):
            for g in range(8):
                def w1step(wi, g=g, half=half):
                    for j in range(4):
                        n = g * 4 + j

                        def ev(b, n=n):
                            act(lambda e: e.activation(out=rtmp, in_=ps[b][:], func=AF.Relu), [f"ps{b}"], ["rtmp"])
                            dve(lambda e: e.tensor_tensor(out=a1T[:, n, :], in0=rtmp, in1=rtmp, op=ALU.mult), ["rtmp"], ["a1T"])
                        gemm_fm(wi, j * 128, lambda kc: hnT[:, kc, :], 512, 16, ["hnT"], ev)
                col = half * 4096 + g * 512
                steps.append(([(w1[:, col:col + 512], 0)], w1step))
            for g in range(4):
                for kq in range(2):
                    def w2step(wi, g=g, kq=kq):
                        for t in range(4):
                            def ev(b, t=t):
                                dve(lambda e: e.tensor_tensor(
                                    out=h_tok[:, t, g * 512:(g + 1) * 512], in0=ps[b][:],
                                    in1=h_tok[:, t, g * 512:(g + 1) * 512], op=ALU.add), [f"ps{b}", f"h{t}"], [f"h{t}"])
                            gemm_tm(wi, 512, lambda kc, t=t: a1T[:, kq * 16 + kc, t * 128:(t + 1) * 128], 16, ["a1T"], ev,
                                    b=t, first=(kq == 0), last=(kq == 1))
                    r0 = half * 4096 + kq * 2048
                    steps.append(([(w2[r0:r0 + 2048, g * 512:(g + 1) * 512], 0)], w2step))

        tapstep(f"h3_{blk}", h_tok[:, :, :], ["h0", "h1", "h2", "h3"])
        def fin(_):
            ot = hnT2.bitcast(F32)[:, 0:2048]
            for t in range(4):
                act(lambda e, t=t: e.activation(out=hs, in_=h_tok[:, t, :], func=AF.Square, accum_out=sm[:, 0:1]),
                    [f"h{t}"], ["hs0", "sm0"])
                rsqrt_small(sm[:, 1:2], sm[:, 0:1], 1.0 / D, ["sm0"], ["sm1"], sm[:, 2:3])
                dve(lambda e, t=t: e.scalar_tensor_tensor(out=ot, in0=h_tok[:, t, :], scalar=sm[:, 1:2], in1=gfr,
                                                          op0=ALU.mult, op1=ALU.mult), [f"h{t}", "sm1", "gfr"], ["hnT"])
                S.dma("sp", lambda e, t=t: e.dma_start(out=outd[ob * TB + t * 128: ob * TB + (t + 1) * 128, :], in_=ot),
                      "d_out", reads=["hnT"], writes=["out"], join=True)
        steps.append((None, fin))

    def conf_chunk(c, cg, ckey, own):
        act(lambda e: e.copy(out=hc[:, c, :], in_=cg[:, 512:542]), [ckey], ["hc"])
        if not own:
            return
        b = gbank()
        for tp in range(31):
            dg = dgs[tp % 12]
            dk = f"dg{tp % 12}"
            wcol = prm[:, O_CFW + c * 31 + tp:O_CFW + c * 31 + tp + 1]
            if tp % 3 == 0:
                dve(lambda e: e.tensor_scalar(out=dg[:], in0=P(O_ID), scalar1=wcol, scalar2=None, op0=ALU.mult), ["prm"], [dk])
            elif tp % 3 == 1:
                act(lambda e: e.activation(out=dg[:], in_=P(O_ID), func=AF.Copy, scale=wcol), ["prm"], [dk])
            else:
                S.op("pool", lambda e: e.tensor_scalar(out=dg[:], in0=P(O_ID), scalar1=wcol, scalar2=1.0, op0=ALU.mult,
                                                       op1=ALU.mult), reads=["prm"], writes=[dk])
            mm(ps[b][:], dg[:], cg[:, tp:tp + 512], tp == 0, tp == 30, [dk, ckey], [f"ps{b}"], True)
        act(lambda e: e.activation(out=cx[:, c, :], in_=ps[b][:], func=AF.Identity, bias=prm[:, O_CFB + c:O_CFB + c + 1]),
            [f"ps{b}", "prm"], ["cx"])
        dve(lambda e: e.tensor_copy(out=cxb, in_=cx[:, c, :]), ["cx"], ["cxb"])
        dve(lambda e: e.tensor_tensor(out=cxq, in0=cx[:, c, :], in1=cx[:, c, :], op=ALU.mult), ["cx"], ["cxq"])
        if c == 0:
            assert 4 in dfree_set and 5 in dfree_set
            dfree_set.discard(4)
            dfree_set.discard(5)
        mm(ps[4][:], onesb[:], cxb, c == 0, c == 7, ["cxb", "onesb"], ["ps4"], True)
        mm(ps[5][:], onesb[:], cxq, c == 0, c == 7, ["cxq", "onesb"], ["ps5"], True)

    def conf_ln():
        dve(lambda e: e.tensor_scalar(out=lnm, in0=ps[4][:], scalar1=1.0 / 1024, scalar2=None, op0=ALU.mult), ["ps4"], ["lnm"])
        dve(lambda e: e.tensor_tensor(out=lnr, in0=lnm, in1=lnm, op=ALU.mult), ["lnm"], ["lnr"])
        dve(lambda e: e.scalar_tensor_tensor(out=lnr, in0=ps[5][:], scalar=1.0 / 1024, in1=lnr, op0=ALU.mult,
                                             op1=ALU.subtract), ["ps5", "lnr"], ["lnr"])
        dfree(4, 5)
        dve(lambda e: e.tensor_scalar(out=lnr, in0=lnr, scalar1=EPS, scalar2=None, op0=ALU.add), ["lnr"], ["lnr"])
        act(lambda e: e.activation(out=lnr, in_=lnr, func=AF.Sqrt), ["lnr"], ["lnr"])
        dve(lambda e: e.reciprocal(out=lnr, in_=lnr), ["lnr"], ["lnr"])
        for c in range(8):
            dve(lambda e, c=c: e.tensor_tensor(out=cx[:, c, :], in0=cx[:, c, :], in1=lnm, op=ALU.subtract), ["cx", "lnm"], ["cx"])
            dve(lambda e, c=c: e.tensor_tensor(out=cx[:, c, :], in0=cx[:, c, :], in1=lnr, op=ALU.mult), ["cx", "lnr"], ["cx"])
            act(lambda e, c=c: e.activation(out=catT[:, 8 + c, :], in_=cx[:, c, :], func=AF.Silu,
                                            bias=prm[:, O_LNB + c:O_LNB + c + 1], scale=prm[:, O_LNG + c:O_LNG + c + 1]),
                ["cx", "prm"], ["catT"])

    def xa_core():
        scale = 512 ** -0.5
        for t in range(4):
            tsl = slice(t * 128, (t + 1) * 128)
            bb = [dbank(), dbank()]
            for hx in range(4):
                b = bb[hx // 2]
                col = (hx % 2) * 256
                for i in range(4):
                    mm(ps[b][:, col:col + 256], xqT[:, 4 * hx + i, tsl], KT[:, 4 * hx + i, :], i == 0, i == 3,
                       ["xqT", "KT"], [f"ps{b}"], i == 3)
            for k in range(2):
                dve(lambda e, k=k: e.tensor_reduce(out=xsm[:, 2 * k:2 * k + 2],
                                                   in_=ps[bb[k]][:].rearrange("p (h m) -> p h m", h=2),
                                                   axis=AX.X, op=ALU.max), [f"ps{bb[k]}"], ["xsm_mx"])
            dve(lambda e: e.tensor_scalar(out=xsm[:, 4:8], in0=xsm[:, 0:4], scalar1=-scale, scalar2=None, op0=ALU.mult),
                ["xsm_mx"], ["xsm_nm"])
            for hx in range(4):
                b = bb[hx // 2]
                col = (hx % 2) * 256
                act(lambda e, b=b, col=col, hx=hx: e.activation(
                    out=ptmp[:, hx, :], in_=ps[b][:, col:col + 256], func=AF.Exp, bias=xsm[:, 4 + hx:5 + hx], scale=scale,
                    accum_out=xsm[:, 8 + hx:9 + hx]), [f"ps{b}", "xsm_nm"], ["ptmp", "xsm_rs"])
            dfree(*bb)
            dve(lambda e: e.reciprocal(out=xsm[:, 12:16], in_=xsm[:, 8:12]), ["xsm_rs"], ["xsm_ri"])
            dve(lambda e: e.tensor_tensor(out=pn, in0=ptmp, in1=xsm[:, 12:16].unsqueeze(2).to_broadcast([128, 4, 256]),
                                          op=ALU.mult), ["ptmp", "xsm_ri"], ["pn"])
            bt = dbank()
            pv = psb[bt][:, 0:1024].rearrange("p (c t) -> p c t", c=8)
            for hx in range(4):
                for mc in range(2):
                    tr(pv[:, hx * 2 + mc, :], pn[:, hx, mc * 128:(mc + 1) * 128], ["pn"], [f"ps{bt}"], hx == 3 and mc == 1)
            act(lambda e, pv=pv, tsl=tsl: e.copy(out=pT[:, :, :, tsl], in_=pv.rearrange("p (h m) t -> p h m t", h=4)),
                [f"ps{bt}"], ["pT"])
            dfree(bt)
        for n in range(16):
            hx = n // 4
            b = gbank()
            for mc in range(2):
                mm(ps[b][:], Vm[:, mc, n * 128:(n + 1) * 128], pT[:, hx, mc, :], mc == 0, mc == 1, ["Vm", "pT"],
                   [f"ps{b}"], mc == 1)
            act(lambda e, b=b, n=n: e.copy(out=xoT[:, n, :], in_=ps[b][:]), [f"ps{b}"], ["xoT"])

    rt8 = sb("rt8", [128, 4, 8], F32)
    pscp = sb("pscp", [128, 64], F32)
    negmb2 = negmb[:, :, :].rearrange("p h e -> p (h e)")

    def dn_group(hg, own, ob):
        h0 = 4 * hg
        sk = lambda n: "sc_" + n
        bc44 = lambda ap: ap.unsqueeze(1).to_broadcast([128, 4, 4])
        act(lambda e: e.activation(out=sc["beta"][:], in_=ba[:, :, h0:h0 + 4], func=AF.Sigmoid), ["ba"], [sk("beta")])
        dve(lambda e: e.tensor_tensor(out=sc["xs"][:], in0=ba[:, :, 8 + h0:12 + h0], in1=bc44(prm[:, O_DTB + h0:O_DTB + h0 + 4]),
                                      op=ALU.add), ["ba", "prm"], [sk("xs")])
        act(lambda e: e.activation(out=sc["axs"][:], in_=sc["xs"][:], func=AF.Abs), [sk("xs")], [sk("axs")])
        act(lambda e: e.activation(out=sc["es"][:], in_=sc["axs"][:], func=AF.Exp, scale=-1.0), [sk("axs")], [sk("es")])
        act(lambda e: e.activation(out=sc["ls"][:], in_=sc["es"][:], func=AF.Ln, bias=prm[:, O_ONE:O_ONE + 1]),
            [sk("es"), "prm"], [sk("ls")])
        dve(lambda e: e.scalar_tensor_tensor(out=sc["xs"][:], in0=sc["xs"][:], scalar=0.0, in1=sc["ls"][:], op0=ALU.max,
                                             op1=ALU.add), [sk("xs"), sk("ls")], [sk("xs")])
        dve(lambda e: e.tensor_tensor(out=sc["graw"][:], in0=sc["xs"][:], in1=bc44(negA[:, h0:h0 + 4]), op=ALU.mult),
            [sk("xs"), "negA"], [sk("graw")])
        dve(lambda e: e.tensor_scalar(out=sc["nbeta"][:], in0=sc["beta"][:], scalar1=-1.0, scalar2=None, op0=ALU.mult),
            [sk("beta")], [sk("nbeta")])
        bs = dbank()
        for t in range(4):
            tsl = slice(t * 128, (t + 1) * 128)
            dve(lambda e: e.tensor_tensor(out=v3(sqt, 8), in0=qkvT[:, 0:8, tsl], in1=qkvT[:, 0:8, tsl], op=ALU.mult),
                ["qkvT"], ["sqt"])
            c0 = t * 16
            for i in range(8):
                mm(ps[bs][:, c0 + i:c0 + i + 1], v3(sqt, 8)[:, i, :], onesb[:, 0:1], True, True, ["sqt", "onesb"], [f"ps{bs}"], False)
            mm(ps[bs][:, c0 + 8:c0 + 12], P(O_U), sc["graw"][:, t, :], True, True, ["prm", sk("graw")], [f"ps{bs}"], False)
            mm(ps[bs][:, c0 + 12:c0 + 16], P(O_ONES), sc["graw"][:, t, :], True, True, ["prm", sk("graw")], [f"ps{bs}"], True)
        act(lambda e: e.copy(out=pscp[:, :], in_=ps[bs][:, 0:64]), [f"ps{bs}"], ["pscp"])
        dfree(bs)
        bs = "cp"
        psv = pscp[:, :].rearrange("p (t n) -> p t n", t=4)
        rsqrt_small(rqk[:, :, :], psv[:, :, 0:8], 1.0, [f"ps{bs}"], ["rqk"], rt8[:, :, :])
        dve(lambda e: e.tensor_scalar(out=sc["rq"][:], in0=rqk[:, :, 0:4], scalar1=128 ** -0.5, scalar2=None, op0=ALU.mult),
            ["rqk"], [sk("rq")])
        act(lambda e: e.copy(out=sc["gcs"][:], in_=psv[:, :, 8:12]), [f"ps{bs}"], [sk("gcs")])
        act(lambda e: e.activation(out=sc["gam"][:], in_=psv[:, :, 8:12], func=AF.Exp), [f"ps{bs}"], [sk("gam")])
        act(lambda e: e.activation(out=sc["eL"][:], in_=psv[:, :, 12:16], func=AF.Exp), [f"ps{bs}"], [sk("eL")])
        dve(lambda e: e.tensor_tensor(out=sc["dl"][:], in0=psv[:, :, 12:16], in1=sc["gcs"][:], op=ALU.subtract),
            [f"ps{bs}", sk("gcs")], [sk("dl")])
        act(lambda e: e.activation(out=sc["edec"][:], in_=sc["dl"][:], func=AF.Exp), [sk("dl")], [sk("edec")])
        dve(lambda e: e.tensor_tensor(out=sc["sB"][:], in0=sc["beta"][:], in1=sc["gam"][:], op=ALU.mult),
            [sk("beta"), sk("gam")], [sk("sB")])
        dve(lambda e: e.tensor_tensor(out=sc["sB"][:], in0=sc["sB"][:], in1=rqk[:, :, 4:8], op=ALU.mult),
            [sk("sB"), "rqk"], [sk("sB")])
        dve(lambda e: e.tensor_tensor(out=sc["sKd"][:], in0=sc["edec"][:], in1=rqk[:, :, 4:8], op=ALU.mult),
            [sk("edec"), "rqk"], [sk("sKd")])
        dve(lambda e: e.tensor_tensor(out=sc["sP1"][:], in0=sc["gam"][:], in1=sc["rq"][:], op=ALU.mult),
            [sk("gam"), sk("rq")], [sk("sP1")])

        def chain(a, b, sfx):
            nh = b - a
            cs = slice(a * 128, b * 128)
            K = lambda n: n + sfx
            V = lambda X: v3(X)[:, a:b, :]
            bch = lambda ap: ap.unsqueeze(2).to_broadcast([128, nh, 128])
            bcm = lambda off: P(off).unsqueeze(1).to_broadcast([128, nh, 128])
            PV = lambda bank: v3(ps[bank][:])[:, a:b, :]
            PVb = lambda bank: v3(psb[bank][:, 0:512])[:, a:b, :]
            for t in range(4):
                tsl = slice(t * 128, (t + 1) * 128)
                g_t = sc["graw"][:, t, a:b]
                S.op("pool", lambda e: e.tensor_tensor(out=V(GU), in0=bcm(O_U), in1=bch(g_t), op=ALU.mult),
                     reads=[sk("graw"), "prm"], writes=[K("GU")])
                bk = dbank()
                for h in range(a, b):
                    tr(v3(psb[bk][:, 0:512])[:, h, :], qkvT[:, 4 + h, tsl], ["qkvT"], [f"ps{bk}"], h == b - 1)
                bv_ = dbank()
                for h in range(a, b):
                    tr(v3(psb[bv_][:, 0:512])[:, h, :], qkvT[:, 8 + h, tsl], ["qkvT"], [f"ps{bv_}"], h == b - 1)
                yield
                dve(lambda e: e.tensor_tensor(out=V(kntok), in0=PVb(bk), in1=bch(rqk[:, t, 4 + a:4 + b]), op=ALU.mult),
                    [f"ps{bk}", "rqk"], [K("kntok")])
                dve(lambda e: e.tensor_tensor(out=V(Bk), in0=PVb(bk), in1=bch(sc["sB"][:, t, a:b]), op=ALU.mult),
                    [f"ps{bk}", sk("sB")], [K("Bk")])
                dve(lambda e: e.tensor_tensor(out=V(kdec), in0=PVb(bk), in1=bch(sc["sKd"][:, t, a:b]), op=ALU.mult),
                    [f"ps{bk}", sk("sKd")], [K("kdec")])
                for h in range(a, b):
                    act(lambda e: e.activation(out=v3(bvv)[:, h, :], in_=v3(psb[bv_][:, 0:512])[:, h, :], func=AF.Copy,
                                               scale=sc["beta"][:, t, h:h + 1]), [f"ps{bv_}", sk("beta")], [K("bvv")])
                dfree(bk, bv_)
                be = dbank()
                mm(ps[be][:, cs], P(O_NEGONES), GU[:, cs], True, False, ["prm", K("GU")], [f"ps{be}"], False)
                mm(ps[be][:, cs], idb[:], negmb2[:, cs], False, True, ["idb", "negmb"], [f"ps{be}"], True)
                yield
                for h in range(a, b):
                    act(lambda e: e.activation(out=v3(Dincl)[:, h, :], in_=ps[be][:, h * 128:(h + 1) * 128], func=AF.Exp,
                                               bias=sc["gcs"][:, t, h:h + 1]), [f"ps{be}", sk("gcs")], [K("Dincl")])
                dfree(be)
                bkt = dbank()
                for h in range(a, b):
                    tr(v3(psb[bkt][:, 0:512])[:, h, :], v3(kntok)[:, h, :], [K("kntok")], [f"ps{bkt}"], h == b - 1)
                yield
                S.op("pool", lambda e: e.tensor_tensor(out=V(Dstr), in0=V(Dincl), in1=bcm(O_STRICT), op=ALU.mult),
                     reads=[K("Dincl"), "prm"], writes=[K("Dstr")])
                act(lambda e: e.copy(out=knT[:, cs], in_=psb[bkt][:, cs]), [f"ps{bkt}"], [K("knT")])
                dfree(bkt)
                yield
                bkk = dbank()
                for h in range(a, b):
                    mm(ps[bkk][:, h * 128:(h + 1) * 128], v3(knT)[:, h, :], v3(knT)[:, h, :], True, True, [K("knT")], [f"ps{bkk}"], h == b - 1)
                if own:
                    bq = dbank()
                    for h in range(a, b):
                        mm(ps[bq][:, h * 128:(h + 1) * 128], qkvT[:, h, tsl], v3(knT)[:, h, :], True, True, ["qkvT", K("knT")],
                           [f"ps{bq}"], h == b - 1)
                yield
                for h in range(a, b):
                    dve(lambda e, h=h: e.scalar_tensor_tensor(
                        out=v3(Nb[0])[:, h, :], in0=ps[bkk][:, h * 128:(h + 1) * 128], scalar=sc["nbeta"][:, t, h:h + 1],
                        in1=v3(Dstr)[:, h, :], op0=ALU.mult, op1=ALU.mult), [f"ps{bkk}", sk("nbeta"), K("Dstr")], [K("Nb0")])
                dfree(bkk)
                if own:
                    for h in range(a, b):
                        dve(lambda e, h=h: e.scalar_tensor_tensor(
                            out=v3(attn)[:, h, :], in0=ps[bq][:, h * 128:(h + 1) * 128], scalar=sc["rq"][:, t, h:h + 1],
                            in1=v3(Dincl)[:, h, :], op0=ALU.mult, op1=ALU.mult), [f"ps{bq}", sk("rq"), K("Dincl")], [K("attn")])
                    dfree(bq)
                yield
                bm = dbank()
                for h in range(a, b):
                    tr(v3(psb[bm][:, 0:512])[:, h, :], v3(Nb[0])[:, h, :], [K("Nb0")], [f"ps{bm}"], h == b - 1)
                if own:
                    bt = dbank()
                    for h in range(a, b):
                        tr(v3(psb[bt][:, 0:512])[:, h, :], v3(attn)[:, h, :], [K("attn")], [f"ps{bt}"], h == b - 1)
                yield
                dve(lambda e: e.tensor_copy(out=Mb[0][:, cs], in_=psb[bm][:, cs]), [f"ps{bm}"], [K("Mb0")])
                dve(lambda e: e.tensor_tensor(out=V(Rb), in0=PVb(bm), in1=bcm(O_ID), op=ALU.add), [f"ps{bm}", "prm"], [K("Rb")])
                dfree(bm)
                if own:
                    act(lambda e: e.copy(out=attnT[:, cs], in_=psb[bt][:, cs]), [f"ps{bt}"], [K("attnT")])
                    dfree(bt)
                yield
                def sq_mms(cur_):
                    bn_ = dbank()
                    for h in range(a, b):
                        mm(ps[bn_][:, h * 128:(h + 1) * 128], v3(Mb[cur_])[:, h, :], v3(Nb[cur_])[:, h, :], True, True,
                           [K(f"Mb{cur_}"), K(f"Nb{cur_}")], [f"ps{bn_}"], h == b - 1)
                    return bn_

                def sqm_mms(cur_):
                    bm_ = dbank()
                    for h in range(a, b):
                        mm(ps[bm_][:, h * 128:(h + 1) * 128], v3(Nb[cur_])[:, h, :], v3(Mb[cur_])[:, h, :], True, True,
                           [K(f"Mb{cur_}"), K(f"Nb{cur_}")], [f"ps{bm_}"], h == b - 1)
                    return bm_

                cur = 0
                bn = sq_mms(0)
                bm2 = sqm_mms(0)
                yield
                act(lambda e: e.copy(out=Nb[1][:, cs], in_=ps[bn][:, cs]), [f"ps{bn}"], [K("Nb1")])
                act(lambda e: e.copy(out=Mb[1][:, cs], in_=ps[bm2][:, cs]), [f"ps{bm2}"], [K("Mb1")])
                dfree(bn, bm2)
                yield
                for lev in range(1, 7):
                    nx = 1 - cur
                    bd = dbank()
                    for h in range(a, b):
                        mm(ps[bd][:, h * 128:(h + 1) * 128], v3(Nb[nx])[:, h, :], v3(Rb)[:, h, :], True, True,
                           [K(f"Nb{nx}"), K("Rb")], [f"ps{bd}"], h == b - 1)
                    if lev < 6:
                        bn = sq_mms(nx)
                        if lev < 5:
                            bm2 = sqm_mms(nx)
                    yield
                    dve(lambda e: e.tensor_tensor(out=Rb[:, cs], in0=ps[bd][:, cs], in1=Rb[:, cs], op=ALU.add),
                        [f"ps{bd}", K("Rb")], [K("Rb")])
                    dfree(bd)
                    if lev < 6:
                        act(lambda e: e.copy(out=Nb[cur][:, cs], in_=ps[bn][:, cs]), [f"ps{bn}"], [K(f"Nb{cur}")])
                        dfree(bn)
                        if lev < 5:
                            act(lambda e: e.copy(out=Mb[cur][:, cs], in_=ps[bm2][:, cs]), [f"ps{bm2}"], [K(f"Mb{cur}")])
                            dfree(bm2)
                    cur = nx
                    yield
                bw = dbank()
                for h in range(a, b):
                    mm(ps[bw][:, h * 128:(h + 1) * 128], v3(Bk)[:, h, :], v3(Rb)[:, h, :], True, True, [K("Bk"), K("Rb")],
                       [f"ps{bw}"], h == b - 1)
                yield
                act(lambda e: e.activation(out=negwT[:, cs], in_=ps[bw][:, cs], func=AF.Copy, scale=-1.0), [f"ps{bw}"], [K("negwT")])
                dfree(bw)
                yield
                bvn = dbank()
                for h in range(a, b):
                    mm(ps[bvn][:, h * 128:(h + 1) * 128], v3(Rb)[:, h, :], v3(bvv)[:, h, :], True, False, [K("Rb"), K("bvv")],
                       [f"ps{bvn}"], False)
                    mm(ps[bvn][:, h * 128:(h + 1) * 128], v3(negwT)[:, h, :], Sbf[:, h0 + h, :], False, True,
                       [K("negwT"), K(f"Sbf{hg}")], [f"ps{bvn}"], h == b - 1)
                if own:
                    bp1 = dbank()
                    for h in range(a, b):
                        mm(ps[bp1][:, h * 128:(h + 1) * 128], qkvT[:, h, tsl], Sbf[:, h0 + h, :], True, True,
                           ["qkvT", K(f"Sbf{hg}")], [f"ps{bp1}"], h == b - 1)
                yield
                act(lambda e: e.copy(out=vnew[:, cs], in_=ps[bvn][:, cs]), [f"ps{bvn}"], [K("vnew")])
                dfree(bvn)
                if own:
                    dve(lambda e: e.tensor_tensor(out=V(tb_), in0=PV(bp1), in1=bch(sc["sP1"][:, t, a:b]), op=ALU.mult),
                        [f"ps{bp1}", sk("sP1")], [K("tb")])
                    dfree(bp1)
                yield
                bs2 = dbank()
                for h in range(a, b):
                    mm(ps[bs2][:, h * 128:(h + 1) * 128], v3(kdec)[:, h, :], v3(vnew)[:, h, :], True, True, [K("kdec"), K("vnew")],
                       [f"ps{bs2}"], h == b - 1)
                if own:
                    bp2 = dbank()
                    for h in range(a, b):
                        mm(ps[bp2][:, h * 128:(h + 1) * 128], v3(attnT)[:, h, :], v3(vnew)[:, h, :], True, True,
                           [K("attnT"), K("vnew")], [f"ps{bp2}"], h == b - 1)
                yield
                for h in range(a, b):
                    dve(lambda e, h=h: e.scalar_tensor_tensor(
                        out=Sst[:, h0 + h, :], in0=Sst[:, h0 + h, :], scalar=sc["eL"][:, t, h:h + 1],
                        in1=ps[bs2][:, h * 128:(h + 1) * 128], op0=ALU.mult, op1=ALU.add),
                        [f"ps{bs2}", sk("eL"), K(f"Sst{hg}")], [K(f"Sst{hg}")])
                dfree(bs2)
                if own:
                    dve(lambda e: e.tensor_tensor(out=yb[:, cs], in0=ps[bp2][:, cs], in1=tb_[:, cs], op=ALU.add),
                        [f"ps{bp2}", K("tb")], [K("yb")])
                    dfree(bp2)
                yield
                act(lambda e: e.copy(out=Sbf[:, h0 + a:h0 + b, :], in_=Sst[:, h0 + a:h0 + b, :]), [K(f"Sst{hg}")], [K(f"Sbf{hg}")])
                if own:
                    dve(lambda e: e.tensor_tensor(out=tb_[:, cs], in0=yb[:, cs], in1=yb[:, cs], op=ALU.mult), [K("yb")], [K("tb")])
                    dve(lambda e: e.tensor_reduce(out=sc["ssy"][:, t, a:b], in_=V(tb_), axis=AX.X, op=ALU.add), [K("tb")], [K("ssy")])
                    yield
                    act(lambda e: e.activation(out=sc["tq"][:, t, a:b], in_=sc["ssy"][:, t, a:b], func=AF.Ln, scale=1.0 / 128,
                                               bias=prm[:, O_EPS:O_EPS + 1]), [K("ssy"), "prm"], [K("tq")])
                    act(lambda e: e.activation(out=sc["ro"][:, t, a:b], in_=sc["tq"][:, t, a:b], func=AF.Exp, scale=-0.5),
                        [K("tq")], [K("ro")])
                    yield
                    dve(lambda e: e.tensor_tensor(out=V(yb), in0=V(yb), in1=bch(sc["ro"][:, t, a:b]), op=ALU.mult),
                        [K("yb"), K("ro")], [K("yb")])
                    dve(lambda e: e.tensor_tensor(out=og[:, cs], in0=yb[:, cs], in1=zs[:, t, (h0 + a) * 128:(h0 + b) * 128], op=ALU.mult),
                        [K("yb"), "zs"], [K("og")])
                    yield
                    bo = dbank()
                    for h in range(a, b):
                        tr(v3(psb[bo][:, 0:512])[:, h, :], v3(og)[:, h, :], [K("og")], [f"ps{bo}"], h == b - 1)
                    yield
                    act(lambda e: e.copy(out=catT[:, h0 + a:h0 + b, tsl], in_=v3(psb[bo][:, 0:512])[:, a:b, :]), [f"ps{bo}"], ["catT"])
                    dfree(bo)
                yield


        if NCHAIN == 4:
            gens = [chain(i, i + 1, "ABCD"[i]) for i in range(4)]
        elif NCHAIN == 2:
            gens = [chain(0, 2, "A"), chain(2, 4, "B")]
        else:
            gens = [chain(0, 4, "A")]
        live = list(gens)
        while live:
            for g in list(live):
                try:
                    next(g)
                except StopIteration:
                    live.remove(g)

    for blk in range(NBLK):
        block(blk)
    run_steps()
    S.final_waits("sp", ["out"] + tapkeys)
    S.emit()
    return nc


def _prm(inp):
    f = np.float32
    p = np.zeros((128, PW), f)
    i = np.arange(128)
    p[:, O_ID:O_ID + 128] = np.eye(128, dtype=f)
    p[:, O_U:O_U + 128] = (i[:, None] <= i[None, :]).astype(f)
    p[:, O_ONES:O_ONES + 128] = 1.0
    p[:, O_NEGONES:O_NEGONES + 128] = -1.0
    p[:, O_STRICT:O_STRICT + 128] = (i[None, :] < i[:, None]).astype(f)
    p[:, O_NEGM:O_NEGM + 128] = np.where(i[None, :] > i[:, None], NEG, 0.0).astype(f)
    col = lambda v: np.ascontiguousarray(np.asarray(v, f).reshape(-1, 128).T)
    p[:, O_GMIX:O_GMIX + 16] = col(inp["norm_mix_g"][0])
    p[:, O_GXA:O_GXA + 16] = col(inp["norm_xa_g"][0])
    p[:, O_GMLP:O_GMLP + 16] = col(inp["norm_mlp_g"][0])
    p[:, O_GMEM:O_GMEM + 16] = col(inp["norm_mem_g"][0])
    p[:, O_DNW:O_DNW + 96] = np.asarray(inp["dn_conv_w"][0], f).reshape(4, 24, 128).transpose(2, 1, 0).reshape(128, 96)
    p[:, O_CFW:O_CFW + 248] = np.asarray(inp["cf_dw_w"][0], f).reshape(31, 8, 128).transpose(2, 1, 0).reshape(128, 248)
    p[:, O_CFB:O_CFB + 8] = col(inp["cf_dw_b"][0])
    p[:, O_LNG:O_LNG + 8] = col(inp["cf_ln_g"][0])
    p[:, O_LNB:O_LNB + 8] = col(inp["cf_ln_b"][0])
    p[:, O_ALOG:O_ALOG + 8] = np.asarray(inp["dn_a_log"][0], f)[None, :]
    p[:, O_DTB:O_DTB + 8] = np.asarray(inp["dn_dt_bias"][0], f)[None, :]
    p[:, O_GN:O_GN + 128] = np.asarray(inp["dn_norm_g"][0], f)[None, :]
    p[:, O_EPS] = EPS
    p[:, O_ONE] = 1.0
    return p


_CACHE = {}


def _get_nc(npre, nown):
    k = (npre, nown)
    if k not in _CACHE:
        _CACHE[k] = build(npre, nown)
    return _CACHE[k]


def make_in_maps(inp, ncores=8, npre=4, nown=4):
    f = np.float32
    x = np.asarray(inp["x"], f)
    mem = np.asarray(inp["mem"], f)
    prm = _prm(inp)
    gfr = np.ascontiguousarray(np.broadcast_to(np.asarray(inp["norm_final_g"], f)[None, :], (128, D)))
    shared = {"prm": prm, "gfr": gfr}
    for k in ("w_in", "w_out", "xa_wq", "xa_wk", "xa_wv", "xa_wo", "mlp_w1", "mlp_w2"):
        shared[k] = np.ascontiguousarray(np.asarray(inp[k], f)[0])
    maps = []
    T2 = nown * TB
    for c in range(ncores):
        b, half = c // 2, c % 2
        own = x[b, half * T2:(half + 1) * T2]
        if npre:
            pre = x[b, 0:npre * TB] if half == 1 else np.zeros((npre * TB, D), f)
            xin = np.concatenate([pre, own], axis=0)
        else:
            xin = own
        m = dict(shared)
        m["xin"] = np.ascontiguousarray(xin)
        m["mem"] = np.ascontiguousarray(mem[b])
        maps.append(m)
    return maps


def kernel(**inputs):
    nc = _get_nc(4, 4)
    maps = make_in_maps(inputs, 8, 4, 4)
    res = run_bass_kernel_spmd(nc, maps, core_ids=list(range(8)))
    out = np.zeros((4, 4096, D), np.float32)
    for c in range(8):
        b, half = c // 2, c % 2
        out[b, half * 2048:(half + 1) * 2048] = res.results[c]["out"]
    return out
```

```python
import numpy as np
import concourse.bass as bass
import concourse.mybir as mybir
from concourse.bass_utils import run_bass_kernel_spmd

F32 = mybir.dt.float32
BF16 = mybir.dt.bfloat16
AF = mybir.ActivationFunctionType
ALU = mybir.AluOpType
AX = mybir.AxisListType

D = 2048
TB = 512
EPS = 1e-6
NEG = -30000.0
O_ID, O_U, O_ONES, O_NEGONES, O_STRICT, O_NEGM = 0, 128, 256, 384, 512, 640
O_GMIX, O_GXA, O_GMLP, O_GMEM = 768, 784, 800, 816
O_DNW, O_CFW, O_CFB, O_LNG, O_LNB = 832, 928, 1176, 1184, 1192
O_ALOG, O_DTB, O_GN, O_EPS, O_ONE = 1200, 1208, 1216, 1344, 1345
PW = 1348
MMLOG = []
import os as _os
NCHAIN = int(_os.environ.get("NCHAIN", "2"))


class _Rec:
    def __getattr__(self, name):
        def f(*a, **k):
            self.call = (name, a, k)
            return self
        return f


class Sched:
    ENG = ("pe", "act", "dve", "pool", "sp")

    def __init__(self, nc):
        self.nc = nc
        self.streams = {e: [] for e in self.ENG}
        self.count = {e: 0 for e in self.ENG}
        self.waited = {}
        self.state = {}
        self.dma_cnt = {}
        self.semnames = set("e_" + e for e in self.ENG)

    def _deps(self, eng, reads, writes, join_sem=None):
        deps = []
        for k in reads:
            st = self.state.get(k)
            if st and st[0]:
                deps.append(st[0])
        for k in writes:
            st = self.state.get(k)
            if st:
                if st[0] and not (join_sem and st[0][0] == join_sem):
                    deps.append(st[0])
                for sn, (v, e) in st[1].items():
                    deps.append((sn, v, e))
        for (sn, v, src) in deps:
            if src == eng and eng == "pe":
                continue
            if self.waited.get((eng, sn), 0) >= v:
                continue
            self.waited[(eng, sn)] = v
            self.streams[eng].append(("wait", sn, v))

    def _commit(self, tok, reads, writes):
        for k in reads:
            st = self.state.setdefault(k, [None, {}])
            old = st[1].get(tok[0])
            if old is None or old[0] < tok[1]:
                st[1][tok[0]] = (tok[1], tok[2])
        for k in writes:
            self.state[k] = [tok, {}]

    def op(self, eng, fn, reads=(), writes=(), signal=True):
        self._deps(eng, reads, writes)
        if signal:
            self.count[eng] += 1
            tok = ("e_" + eng, self.count[eng], eng)
        else:
            tok = ("e_" + eng, self.count[eng] + 1, eng)
        rec = _Rec()
        fn(rec)
        self.streams[eng].append(("op", rec.call, ("e_" + eng, 1) if signal else None))
        self._commit(tok, reads, writes)

    def dma(self, q, fn, dsem, reads=(), writes=(), join=False):
        self.semnames.add(dsem)
        self._deps(q, reads, writes, join_sem=dsem if join else None)
        self.dma_cnt[dsem] = self.dma_cnt.get(dsem, 0) + 16
        tok = (dsem, self.dma_cnt[dsem], "dma")
        rec = _Rec()
        fn(rec)
        self.streams[q].append(("op", rec.call, (dsem, 16)))
        self._commit(tok, reads, writes)

    def fence(self, new_keys, old_keys):
        merged = {}
        for k in old_keys:
            st = self.state.get(k)
            if not st:
                continue
            toks = list(st[1].items())
            if st[0]:
                toks.append((st[0][0], (st[0][1], st[0][2])))
            for sn, (v, e) in toks:
                if sn not in merged or merged[sn][0] < v:
                    merged[sn] = (v, e)
        for k in new_keys:
            st = self.state.setdefault(k, [None, {}])
            for sn, ve in merged.items():
                if sn not in st[1] or st[1][sn][0] < ve[0]:
                    st[1][sn] = ve

    def final_waits(self, eng, keys):
        self._deps(eng, keys, ())

    def emit(self):
        nc = self.nc
        sems = {n: nc.alloc_semaphore(n) for n in sorted(self.semnames)}
        handles = {"pe": "tensor", "act": "scalar", "dve": "vector", "pool": "gpsimd", "sp": "sync"}
        with nc.Block() as block:
            for e in self.ENG:
                items = self.streams[e]

                def body(engh, items=items):
                    for it in items:
                        if it[0] == "wait":
                            engh.wait_ge(sems[it[1]], it[2])
                        else:
                            ins = getattr(engh, it[1][0])(*it[1][1], **it[1][2])
                            if it[2] is not None:
                                ins.then_inc(sems[it[2][0]], it[2][1])

                getattr(block, handles[e])(body)


def build(NPRE, NOWN, taps=()):
    nc = bass.Bass("TRN2", target_bir_lowering=False)
    S = Sched(nc)
    NBLK = NPRE + NOWN

    def din(name, shape):
        return nc.dram_tensor(name, list(shape), F32, kind="ExternalInput").ap()

    xin = din("xin", [NBLK * TB, D])
    memd = din("mem", [256, D])
    prmd = din("prm", [128, PW])
    gfrd = din("gfr", [128, D])
    w_in = din("w_in", [D, 6160])
    w_out = din("w_out", [D, D])
    wq = din("xa_wq", [D, D])
    wk = din("xa_wk", [D, D])
    wv = din("xa_wv", [D, D])
    wo = din("xa_wo", [D, D])
    w1 = din("mlp_w1", [D, 8192])
    w2 = din("mlp_w2", [8192, D])
    outd = nc.dram_tensor("out", [NOWN * TB, D], F32, kind="ExternalOutput").ap()
    tapd = {}

    def sb(name, shape, dt):
        return nc.alloc_sbuf_tensor("s_" + name, list(shape), dt)

    prm = sb("prm", [128, PW], F32)
    idb = sb("idb", [128, 128], BF16)
    onesb = sb("onesb", [128, 128], BF16)
    negmb = sb("negmb", [128, 4, 128], BF16)
    negA = sb("negA", [128, 8], F32)
    h_tok = sb("h_tok", [128, 4, D], F32)
    hnT2 = sb("hnT", [128, 16 * TB], BF16)
    hnT = hnT2[:, :].rearrange("p (c t) -> p c t", c=16)
    NW = 2
    wb = [sb(f"wb{i}", [128, 16, 512], BF16) for i in range(NW)]
    KT = sb("KT", [128, 16, 256], BF16)
    Vm = sb("Vm", [128, 2, D], BF16)
    Sst = sb("Sst", [128, 8, 128], F32)
    Sbf = sb("Sbf", [128, 8, 128], BF16)
    hq = sb("hq", [128, 24, 3], BF16)
    hc = sb("hc", [128, 8, 30], BF16)
    ba = sb("ba", [128, 4, 16], F32)
    sm = sb("sm", [128, 64], F32)
    rawc = [sb(f"rawc{i}", [128, 515], BF16) for i in range(2)]
    cgl = [sb(f"cgl{i}", [128, 542], BF16) for i in range(2)]
    dgs = [sb(f"dg{i}", [128, 128], BF16) for i in range(12)]
    RA = sb("RA", [128, 20480], BF16)
    RAf = RA.bitcast(F32)
    RB = sb("RB", [128, 16384], BF16)
    RBf = RB.bitcast(F32)

    qkvT = RA[:, 0:6144].rearrange("p (c t) -> p c t", c=12)
    cx = RAf[:, 0:4096].rearrange("p (c t) -> p c t", c=8)
    zs = RA[:, 8192:12288].rearrange("p (t n) -> p t n", t=4)
    catT = RA[:, 12288:20480].rearrange("p (c t) -> p c t", c=16)
    xqT = RA[:, 0:8192].rearrange("p (c t) -> p c t", c=16)
    pT = RA[:, 8192:12288].rearrange("p (h m t) -> p h m t", h=4, m=2)
    xoT = RA[:, 12288:20480].rearrange("p (c t) -> p c t", c=16)
    gfr = RAf[:, 0:2048]
    A_MIX = ["qkvT", "cx", "zs", "catT", "cxb", "cxq", "lnm", "lnr"]
    A_XA = ["xqT", "pT", "xoT"]
    A_MLP = ["gfr"]
    a1T = RB[:, 0:16384].rearrange("p (c t) -> p c t", c=32)
    hs = sb("hs", [128, 2048], BF16)[:, :]
    hsb = [hs, sb("hs1", [128, 2048], BF16)[:, :]]
    ptmp = RBf[:, 0:1024].rearrange("p (h m) -> p h m", h=4)
    pn = RB[:, 2048:3072].rearrange("p (h m) -> p h m", h=4)
    rtmp = sb("rtmp", [128, 512], BF16)[:, :]
    tmpa = sb("tmpa", [128, 512], BF16)[:, :]
    tmps = sb("tmps", [128, 512], BF16)[:, :]
    tmpa2 = sb("tmpa2", [128, 512], BF16)[:, :]
    tmps2 = sb("tmps2", [128, 512], BF16)[:, :]

    def v3(ap2, h=4):
        return ap2.rearrange("p (h e) -> p h e", h=h)

    o = 0

    def bvb():
        nonlocal o
        r = RB[:, o:o + 512]
        o += 512
        return r

    def bvf():
        nonlocal o
        r = RBf[:, o // 2:o // 2 + 512]
        o += 1024
        return r
    Dincl = bvf(); Dstr = bvf(); Gb = bvf(); GU = bvf()
    Nb = [bvb(), bvb()]; Mb = [bvb(), bvb()]
    Rf = bvf(); Rb = bvb()
    kntok = bvb(); Bk = bvb(); kdec = bvb(); bvv = bvb()
    knT = bvb(); attn = bvb(); attnT = bvb(); negwT = bvb(); vnew = bvb()
    yb = bvf(); tb_ = bvf(); og = bvb()
    sqy = tb_
    sqt = RB[:, o:o + 1024]; o += 1024
    cxb = RA[:, 8192:8704]; cxq = RA[:, 8704:9216]; lnm = RAf[:, 4608:5120]; lnr = RAf[:, 5120:5632]
    assert o <= 16384, o
    B_DN0 = ["Dincl", "Dstr", "Gb", "GU", "Nb0", "Nb1", "Mb0", "Mb1", "Rf", "Rb", "kntok", "Bk", "kdec", "bvv",
             "knT", "attn", "attnT", "negwT", "vnew", "yb", "tb", "og"]
    B_DN = [k + s for k in B_DN0 for s in ("A", "B", "C", "D")] + ["sqt"]
    B_XA = ["ptmp", "pn"]
    B_MLP = ["a1T"]
    scn = ["beta", "graw", "xs", "axs", "es", "ls", "rq", "rk", "gcs", "gam", "eL", "dl", "edec", "sB", "sKd",
           "nbeta", "sP1", "ssy", "ro", "tq"]
    sc = {n: sb("sc_" + n, [128, 4, 4], F32) for n in scn}
    rqk = sb("rqk", [128, 4, 8], F32)
    xsm = sb("xsm", [128, 16], F32)

    ps = [nc.alloc_psum_tensor(f"ps{i}", [128, 512], F32) for i in range(8)]
    psb = [p.bitcast(BF16) for p in ps]
    gctr = [0]
    dctr = [0]

    def gbank():
        i = gctr[0] % 4
        gctr[0] += 1
        return i

    dfree_set = {0, 1, 2, 3, 4, 5, 6, 7}

    def dbank():
        for k in range(8):
            i = (4 + dctr[0] + k) % 8
            if i in dfree_set:
                dfree_set.discard(i)
                dctr[0] = (i - 4) % 8 + 1
                return i
        raise RuntimeError("no free DN psum bank")

    def dfree(*bs_):
        for i in bs_:
            assert i not in dfree_set
            dfree_set.add(i)

    P = lambda off, n=128: prm[:, off:off + n]

    def act(fn, reads, writes):
        S.op("act", fn, reads=reads, writes=writes)

    def dve(fn, reads, writes):
        S.op("dve", fn, reads=reads, writes=writes)

    def mm(out, lhsT, rhs, start, stop, reads, writes, signal):
        MMLOG.append((writes[0], bool(start), bool(stop), str(out)[:80]))
        S.op("pe", lambda e: e.matmul(out, lhsT=lhsT, rhs=rhs, start=start, stop=stop),
             reads=reads, writes=writes, signal=signal)

    def tr(out, in_, reads, writes, signal):
        MMLOG.append((writes[0], "T", "T", ""))
        S.op("pe", lambda e: e.transpose(out=out, in_=in_, identity=idb[:]),
             reads=list(reads) + ["idb"], writes=writes, signal=signal)

    def rsqrt_small(dst, src, scale, rk, wk, tmp):
        act(lambda e: e.activation(out=tmp, in_=src, func=AF.Ln, scale=scale, bias=prm[:, O_EPS:O_EPS + 1]),
            list(rk) + ["prm"], ["rs_tmp"])
        act(lambda e: e.activation(out=dst, in_=tmp, func=AF.Exp, scale=-0.5), ["rs_tmp"], wk)

    wctr = [0]

    def load_w(parts):
        i = wctr[0] % NW
        wctr[0] += 1
        for (src, co) in parts:
            n = src.shape[1]
            v = src.rearrange("(kc p) n -> p kc n", p=128)
            for q4 in range(4):
                S.dma("pool", lambda e, i=i, v=v, q4=q4, co=co, n=n: e.dma_start(
                    out=wb[i][:, 4 * q4:4 * q4 + 4, co:co + n], in_=v[:, 4 * q4:4 * q4 + 4, :]),
                    f"d_w{i}", writes=[f"w{i}"], join=True)
        return i

    steps = []
    tapkeys = []

    def tap(name, ap, keys):
        if name not in taps:
            return
        d = nc.dram_tensor("tap_" + name, list(ap.shape), ap.dtype, kind="ExternalOutput").ap()
        S.dma("sp", lambda e: e.dma_start(out=d, in_=ap), "d_tap_" + name, reads=keys, writes=["tap_" + name])
        tapkeys.append("tap_" + name)

    def tapstep(name, ap, keys):
        steps.append((None, lambda _: tap(name, ap, keys)))

    def run_steps():
        widx = [i for i, s in enumerate(steps) if s[0] is not None]
        slot = {}
        nxt = [0]

        def ensure(n_ahead_of):
            while nxt[0] < len(widx) and nxt[0] <= n_ahead_of:
                slot[widx[nxt[0]]] = load_w(steps[widx[nxt[0]]][0])
                nxt[0] += 1

        pos = 0
        ensure(0)
        for i, (parts, fn) in enumerate(steps):
            if parts is not None:
                ensure(pos + NW - 1)
                fn(slot[i])
                pos += 1
            else:
                fn(None)

    S.dma("sp", lambda e: e.dma_start(out=prm[:], in_=prmd), "d_prm", writes=["prm"])
    act(lambda e: e.copy(out=idb[:], in_=P(O_ID)), ["prm"], ["idb"])
    act(lambda e: e.copy(out=onesb[:], in_=P(O_ONES)), ["prm"], ["onesb"])
    for h in range(4):
        act(lambda e, h=h: e.copy(out=negmb[:, h, :], in_=P(O_NEGM)), ["prm"], ["negmb"])
    act(lambda e: e.activation(out=negA[:], in_=P(O_ALOG, 8), func=AF.Exp), ["prm"], ["negA"])
    dve(lambda e: e.tensor_scalar(out=negA[:], in0=negA[:], scalar1=-1.0, scalar2=None, op0=ALU.mult), ["negA"], ["negA"])
    dve(lambda e: e.memset(Sst[:], 0.0), [], [f"Sst{i}{s}" for i in range(2) for s in "ABCD"])
    dve(lambda e: e.memset(Sbf[:], 0.0), [], [f"Sbf{i}{s}" for i in range(2) for s in "ABCD"])
    dve(lambda e: e.memset(hq[:], 0.0), [], ["hq"])
    dve(lambda e: e.memset(hc[:], 0.0), [], ["hc"])
    dve(lambda e: e.memset(RB[:, 0:16384], 0.0), [], B_DN)
    dve(lambda e: e.memset(RA[:, :], 0.0), [], A_MIX + A_XA + A_MLP)

    def norm_T(src_fn, src_keys, gcol_off, dst, dst_key, ntile, tok_w):
        def part1(t):
            src = src_fn(t)
            hb, hk = hsb[t % 2], f"hs{t % 2}"
            act(lambda e: e.activation(out=hb, in_=src, func=AF.Square, accum_out=sm[:, 0:1]),
                src_keys(t), [hk, "sm0"])
            rsqrt_small(sm[:, 1:2], sm[:, 0:1], 1.0 / D, ["sm0"], ["sm1"], sm[:, 2:3])
            act(lambda e: e.activation(out=hb, in_=src, func=AF.Copy, scale=sm[:, 1:2]),
                src_keys(t) + ["sm1"], [hk])

        def part2(t):
            hb, hk = hsb[t % 2], f"hs{t % 2}"
            for half in range(2):
                b = dbank()
                pv = psb[b][:, 0:1024].rearrange("p (c t) -> p c t", c=8)
                for c in range(8):
                    kc = half * 8 + c
                    tr(pv[:, c, :], hb[:, kc * 128:(kc + 1) * 128], [hk], [f"ps{b}"], c == 7)
                dve(lambda e: e.tensor_tensor(
                    out=dst[:, half * 8:half * 8 + 8, t * 128:(t + 1) * 128], in0=pv,
                    in1=prm[:, gcol_off + half * 8:gcol_off + half * 8 + 8].unsqueeze(2).to_broadcast([128, 8, 128]),
                    op=ALU.mult), [f"ps{b}", "prm"], [dst_key])
                dfree(b)

        for t in range(ntile):
            part1(t)
            if t > 0:
                part2(t - 1)
        part2(ntile - 1)

    def gemm_fm(wi, wcol, rhs_fn, n, nk, rk, evac):
        b = gbank()
        for kc in range(nk):
            mm(ps[b][:, 0:n], wb[wi][:, kc, wcol:wcol + 128], rhs_fn(kc), kc == 0, kc == nk - 1,
               [f"w{wi}"] + rk, [f"ps{b}"], kc == nk - 1)
        evac(b)

    def gemm_tm(wi, ncols, lhs_fn, nk, rk, evac, b=None, first=True, last=True):
        if b is None:
            b = gbank()
        for kc in range(nk):
            mm(ps[b][:, 0:ncols], lhs_fn(kc), wb[wi][:, kc, 0:ncols], first and kc == 0, last and kc == nk - 1,
               [f"w{wi}"] + rk, [f"ps{b}"], kc == nk - 1)
        if last:
            evac(b)
        return b

    memt = [h_tok[:, 0, :], h_tok[:, 1, :]]

    def mem_prep(_):
        for t in range(2):
            S.dma("sp", lambda e, t=t: e.dma_start(out=memt[t], in_=memd[t * 128:(t + 1) * 128, :]),
                  f"d_x{t}", writes=[f"h{t}"])
        norm_T(lambda t: memt[t], lambda t: [f"h{t}"], O_GMEM, hnT, "hnT", 2, 256)
    steps.append((None, mem_prep))
    for g in range(4):
        def kstep(wi, g=g):
            for j in range(4):
                n = g * 4 + j
                gemm_fm(wi, j * 128, lambda kc: hnT[:, kc, 0:256], 256, 16, ["hnT"],
                        lambda b, n=n: act(lambda e: e.copy(out=KT[:, n, :], in_=ps[b][:, 0:256]), [f"ps{b}"], ["KT"]))
        steps.append(([(wk[:, g * 512:(g + 1) * 512], 0)], kstep))
    for g in range(4):
        def vstep(wi, g=g):
            for t in range(2):
                gemm_tm(wi, 512, lambda kc, t=t: hnT[:, kc, t * 128:(t + 1) * 128], 16, ["hnT"],
                        lambda b, t=t: act(lambda e: e.copy(out=Vm[:, t, g * 512:(g + 1) * 512], in_=ps[b][:]),
                                           [f"ps{b}"], ["Vm"]))
        steps.append(([(wv[:, g * 512:(g + 1) * 512], 0)], vstep))

    def block(blk):
        own = blk >= NPRE
        need_qg = own or blk == NPRE - 1
        ob = blk - NPRE

        def load_x(_):
            S.fence(A_MIX, A_MLP + A_XA)
            S.fence(B_DN + ["hs"], B_MLP + B_XA)
            for t in range(4):
                S.dma("sp", lambda e, t=t: e.dma_start(out=h_tok[:, t, :],
                                                       in_=xin[blk * TB + t * 128: blk * TB + (t + 1) * 128, :]),
                      f"d_x{t}", writes=[f"h{t}"])
            norm_T(lambda t: h_tok[:, t, :], lambda t: [f"h{t}"], O_GMIX, hnT, "hnT", 4, TB)
        steps.append((None, load_x))
        tapstep(f"hnT{blk}", hnT2[:, :], ["hnT"])

        if own:
            for g in range(2):
                def zstep(wi, g=g):
                    for t in range(4):
                        def ev(b, t=t):
                            act(lambda e: e.activation(out=hs[:, 0:512], in_=ps[b][:], func=AF.Silu), [f"ps{b}"], ["hs0"])
                            dve(lambda e: e.tensor_tensor(
                                out=zs[:, t, g * 512:(g + 1) * 512].rearrange("p (h e) -> p h e", h=4),
                                in0=hs[:, 0:512].rearrange("p (h e) -> p h e", h=4),
                                in1=prm[:, O_GN:O_GN + 128].unsqueeze(1).to_broadcast([128, 4, 128]), op=ALU.mult),
                                ["hs0", "prm"], ["zs"])
                        gemm_tm(wi, 512, lambda kc, t=t: hnT[:, kc, t * 128:(t + 1) * 128], 16, ["hnT"], ev)
                steps.append(([(w_in[:, 3072 + g * 512:3072 + (g + 1) * 512], 0)], zstep))

        def bastep(wi):
            for t in range(4):
                gemm_tm(wi, 16, lambda kc, t=t: hnT[:, kc, t * 128:(t + 1) * 128], 16, ["hnT"],
                        lambda b, t=t: act(lambda e: e.copy(out=ba[:, t, :], in_=ps[b][:, 0:16]), [f"ps{b}"], ["ba"]))
        steps.append(([(w_in[:, 4096:4112], 0)], bastep))
        tapstep(f"ba{blk}", ba[:, :, :], ["ba"])
        tapstep(f"zs{blk}", RA[:, 8192:12288], ["zs"])

        for hg in range(2):
            for part in range(3):
                if part == 0 and not need_qg:
                    continue

                def qkvstep(wi, part=part, hg=hg):
                    def conv(j):
                        c24 = part * 8 + hg * 4 + j
                        cl = part * 4 + j
                        rc = rawc[c24 % 2]
                        rkey = f"rawc{c24 % 2}"
                        act(lambda e: e.copy(out=hq[:, c24, :], in_=rc[:, 512:515]), [rkey], ["hq"])
                        b2 = gbank()
                        for tp in range(4):
                            dg = dgs[(c24 * 4 + tp) % 12]
                            dk = f"dg{(c24 * 4 + tp) % 12}"
                            act(lambda e: e.activation(
                                out=dg[:], in_=P(O_ID), func=AF.Copy, scale=prm[:, O_DNW + c24 * 4 + tp:O_DNW + c24 * 4 + tp + 1]),
                                ["prm"], [dk])
                            mm(ps[b2][:], dg[:], rc[:, tp:tp + 512], tp == 0, tp == 3, [dk, rkey], [f"ps{b2}"], tp == 3)
                        act(lambda e: e.activation(out=qkvT[:, cl, :], in_=ps[b2][:], func=AF.Silu), [f"ps{b2}"], ["qkvT"])

                    if part == 0 and not own:
                        for j in range(4):
                            c24 = part * 8 + hg * 4 + j
                            rc = rawc[c24 % 2]
                            rkey = f"rawc{c24 % 2}"
                            gemm_fm(wi, j * 128, lambda kc: hnT[:, kc, 384:512], 128, 16, ["hnT"],
                                    lambda b: act(lambda e: e.copy(out=rc[:, 387:515], in_=ps[b][:, 0:128]), [f"ps{b}"], [rkey]))
                            act(lambda e: e.copy(out=hq[:, c24, :], in_=rc[:, 512:515]), [rkey], ["hq"])
                        return
                    for j in range(4):
                        c24 = part * 8 + hg * 4 + j
                        rc = rawc[c24 % 2]
                        rkey = f"rawc{c24 % 2}"
                        act(lambda e: e.copy(out=rc[:, 0:3], in_=hq[:, c24, :]), ["hq"], [rkey])
                        gemm_fm(wi, j * 128, lambda kc: hnT[:, kc, :], 512, 16, ["hnT"],
                                lambda b: act(lambda e: e.copy(out=rc[:, 3:515], in_=ps[b][:]), [f"ps{b}"], [rkey]))
                        if j > 0:
                            conv(j - 1)
                    conv(3)
                col = part * 1024 + hg * 512
                steps.append(([(w_in[:, col:col + 512], 0)], qkvstep))
            tapstep(f"qkvT{blk}_{hg}", RA[:, 0:6144], ["qkvT"])
            steps.append((None, lambda _, hg=hg: dn_group(hg, own, ob)))
            tapstep(f"S{blk}_{hg}", Sst[:, :, :], [f"Sst{i}{s}" for i in range(2) for s in "ABCD"])
            tapstep(f"cat{blk}_{hg}", RA[:, 12288:20480], ["catT"])

        if need_qg:
            for g in range(4):
                def glustep(wi, g=g):
                    if g == 0 and own:
                        S.fence(["cx"], ["qkvT"])
                        S.fence(["cxb", "cxq", "lnm", "lnr"], ["zs"])
                    if not own:
                        for j in range(2):
                            c = g * 2 + j
                            cg = cgl[c % 2]
                            ckey = f"cgl{c % 2}"
                            gemm_fm(wi, j * 128, lambda kc: hnT[:, kc, 384:512], 128, 16, ["hnT"],
                                    lambda b: act(lambda e: e.copy(out=tmpa[:, 0:128], in_=ps[b][:, 0:128]), [f"ps{b}"], ["tmpa"]))
                            gemm_fm(wi, 256 + j * 128, lambda kc: hnT[:, kc, 384:512], 128, 16, ["hnT"],
                                    lambda b: act(lambda e: e.activation(out=tmps[:, 0:128], in_=ps[b][:, 0:128], func=AF.Sigmoid),
                                                  [f"ps{b}"], ["tmps"]))
                            dve(lambda e: e.tensor_tensor(out=cg[:, 414:542], in0=tmpa[:, 0:128], in1=tmps[:, 0:128], op=ALU.mult),
                                ["tmpa", "tmps"], [ckey])
                            conf_chunk(c, cg, ckey, own)
                        return
                    for j in range(2):
                        c = g * 2 + j
                        cg = cgl[c % 2]
                        ckey = f"cgl{c % 2}"
                        ta, ts_ = (tmpa, tmps) if j == 0 else (tmpa2, tmps2)
                        ka, ks = ("tmpa", "tmps") if j == 0 else ("tmpa2", "tmps2")
                        gemm_fm(wi, j * 128, lambda kc: hnT[:, kc, :], 512, 16, ["hnT"],
                                lambda b: act(lambda e: e.copy(out=ta, in_=ps[b][:]), [f"ps{b}"], [ka]))
                        gemm_fm(wi, 256 + j * 128, lambda kc: hnT[:, kc, :], 512, 16, ["hnT"],
                                lambda b: act(lambda e: e.activation(out=ts_, in_=ps[b][:], func=AF.Sigmoid),
                                              [f"ps{b}"], [ks]))
                        act(lambda e: e.copy(out=cg[:, 0:30], in_=hc[:, c, :]), ["hc"], [ckey])
                        dve(lambda e: e.tensor_tensor(out=cg[:, 30:542], in0=ta, in1=ts_, op=ALU.mult), [ka, ks], [ckey])
                    for j in range(2):
                        c = g * 2 + j
                        conf_chunk(c, cgl[c % 2], f"cgl{c % 2}", own)
                steps.append(([(w_in[:, 4112 + g * 256:4112 + (g + 1) * 256], 0),
                               (w_in[:, 5136 + g * 256:5136 + (g + 1) * 256], 256)], glustep))
        if not own:
            return
        steps.append((None, lambda _: conf_ln()))
        tapstep(f"catc{blk}", RA[:, 12288:20480], ["catT"])

        def resid_step(src, skeys):
            def f(wi, g):
                for t in range(4):
                    gemm_tm(wi, 512, lambda kc, t=t: src[:, kc, t * 128:(t + 1) * 128], 16, skeys,
                            lambda b, t=t: dve(lambda e: e.tensor_tensor(
                                out=h_tok[:, t, g * 512:(g + 1) * 512], in0=ps[b][:], in1=h_tok[:, t, g * 512:(g + 1) * 512],
                                op=ALU.add), [f"ps{b}", f"h{t}"], [f"h{t}"]))
            return f
        for g in range(4):
            steps.append(([(w_out[:, g * 512:(g + 1) * 512], 0)], lambda wi, g=g: resid_step(catT, ["catT"])(wi, g)))

        tapstep(f"h1_{blk}", h_tok[:, :, :], ["h0", "h1", "h2", "h3"])
        def xa_norm(_):
            S.fence(A_XA, A_MIX)
            S.fence(B_XA, B_DN)
            norm_T(lambda t: h_tok[:, t, :], lambda t: [f"h{t}"], O_GXA, hnT, "hnT", 4, TB)
        steps.append((None, xa_norm))
        for g in range(4):
            def qstep(wi, g=g):
                for j in range(4):
                    n = g * 4 + j
                    gemm_fm(wi, j * 128, lambda kc: hnT[:, kc, :], 512, 16, ["hnT"],
                            lambda b, n=n: act(lambda e: e.copy(out=xqT[:, n, :], in_=ps[b][:]), [f"ps{b}"], ["xqT"]))
            steps.append(([(wq[:, g * 512:(g + 1) * 512], 0)], qstep))
        steps.append((None, lambda _: xa_core()))
        for g in range(4):
            steps.append(([(wo[:, g * 512:(g + 1) * 512], 0)], lambda wi, g=g: resid_step(xoT, ["xoT"])(wi, g)))

        tapstep(f"h2_{blk}", h_tok[:, :, :], ["h0", "h1", "h2", "h3"])
        def mlp_norm(_):
            S.fence(A_MLP, A_XA)
            S.fence(B_MLP, B_XA + B_DN + ["hs"])
            norm_T(lambda t: h_tok[:, t, :], lambda t: [f"h{t}"], O_GMLP, hnT, "hnT", 4, TB)
            S.dma("sp", lambda e: e.dma_start(out=gfr, in_=gfrd), "d_gfr", writes=["gfr"])
        steps.append((None, mlp_norm))
        for half in range(2):
            for g in range(8):
                def w1step(wi, g=g, half=half):
                    for j in range(4):
                        n = g * 4 + j

                        def ev(b, n=n):
                            act(lambda e: e.activation(out=rtmp, in_=ps[b][:], func=AF.Relu), [f"ps{b}"], ["rtmp"])
                            dve(lambda e: e.tensor_tensor(out=a1T[:, n, :], in0=rtmp, in1=rtmp, op=ALU.mult), ["rtmp"], ["a1T"])
                        gemm_fm(wi, j * 128, lambda kc: hnT[:, kc, :], 512, 16, ["hnT"], ev)
                col = half * 4096 + g * 512
                steps.append(([(w1[:, col:col + 512], 0)], w1step))
            for g in range(4):
                for kq in range(2):
                    def w2step(wi, g=g, kq=kq):
                        for t in range(4):
                            def ev(b, t=t):
                                dve(lambda e: e.tensor_tensor(
                                    out=h_tok[:, t, g * 512:(g + 1) * 512], in0=ps[b][:],
                                    in1=h_tok[:, t, g * 512:(g + 1) * 512], op=ALU.add), [f"ps{b}", f"h{t}"], [f"h{t}"])
                            gemm_tm(wi, 512, lambda kc, t=t: a1T[:, kq * 16 + kc, t * 128:(t + 1) * 128], 16, ["a1T"], ev,
                                    b=t, first=(kq == 0), last=(kq == 1))
                    r0 = half * 4096 + kq * 2048
                    steps.append(([(w2[r0:r0 + 2048, g * 512:(g + 1) * 512], 0)], w2step))

        tapstep(f"h3_{blk}", h_tok[:, :, :], ["h0", "h1", "h2", "h3"])
        def fin(_):
            ot = hnT2.bitcast(F32)[:, 0:2048]
            for t in range(4):
                act(lambda e, t=t: e.activation(out=hs, in_=h_tok[:, t, :], func=AF.Square, accum_out=sm[:, 0:1]),
                    [f"h{t}"], ["hs0", "sm0"])
                rsqrt_small(sm[:, 1:2], sm[:, 0:1], 1.0 / D, ["sm0"], ["sm1"], sm[:, 2:3])
                dve(lambda e, t=t: e.scalar_tensor_tensor(out=ot, in0=h_tok[:, t, :], scalar=sm[:, 1:2], in1=gfr,
                                                          op0=ALU.mult, op1=ALU.mult), [f"h{t}", "sm1", "gfr"], ["hnT"])
                S.dma("sp", lambda e, t=t: e.dma_start(out=outd[ob * TB + t * 128: ob * TB + (t + 1) * 128, :], in_=ot),
                      "d_out", reads=["hnT"], writes=["out"], join=True)
        steps.append((None, fin))

    def conf_chunk(c, cg, ckey, own):
        act(lambda e: e.copy(out=hc[:, c, :], in_=cg[:, 512:542]), [ckey], ["hc"])
        if not own:
            return
        b = gbank()
        for tp in range(31):
            dg = dgs[tp % 12]
            dk = f"dg{tp % 12}"
            wcol = prm[:, O_CFW + c * 31 + tp:O_CFW + c * 31 + tp + 1]
            if tp % 3 == 0:
                dve(lambda e: e.tensor_scalar(out=dg[:], in0=P(O_ID), scalar1=wcol, scalar2=None, op0=ALU.mult), ["prm"], [dk])
            elif tp % 3 == 1:
                act(lambda e: e.activation(out=dg[:], in_=P(O_ID), func=AF.Copy, scale=wcol), ["prm"], [dk])
            else:
                S.op("pool", lambda e: e.tensor_scalar(out=dg[:], in0=P(O_ID), scalar1=wcol, scalar2=1.0, op0=ALU.mult,
                                                       op1=ALU.mult), reads=["prm"], writes=[dk])
            mm(ps[b][:], dg[:], cg[:, tp:tp + 512], tp == 0, tp == 30, [dk, ckey], [f"ps{b}"], True)
        act(lambda e: e.activation(out=cx[:, c, :], in_=ps[b][:], func=AF.Identity, bias=prm[:, O_CFB + c:O_CFB + c + 1]),
            [f"ps{b}", "prm"], ["cx"])
        dve(lambda e: e.tensor_copy(out=cxb, in_=cx[:, c, :]), ["cx"], ["cxb"])
        dve(lambda e: e.tensor_tensor(out=cxq, in0=cx[:, c, :], in1=cx[:, c, :], op=ALU.mult), ["cx"], ["cxq"])
        if c == 0:
            assert 4 in dfree_set and 5 in dfree_set
            dfree_set.discard(4)
            dfree_set.discard(5)
        mm(ps[4][:], onesb[:], cxb, c == 0, c == 7, ["cxb", "onesb"], ["ps4"], True)
        mm(ps[5][:], onesb[:], cxq, c == 0, c == 7, ["cxq", "onesb"], ["ps5"], True)

    def conf_ln():
        dve(lambda e: e.tensor_scalar(out=lnm, in0=ps[4][:], scalar1=1.0 / 1024, scalar2=None, op0=ALU.mult), ["ps4"], ["lnm"])
        dve(lambda e: e.tensor_tensor(out=lnr, in0=lnm, in1=lnm, op=ALU.mult), ["lnm"], ["lnr"])
        dve(lambda e: e.scalar_tensor_tensor(out=lnr, in0=ps[5][:], scalar=1.0 / 1024, in1=lnr, op0=ALU.mult,
                                             op1=ALU.subtract), ["ps5", "lnr"], ["lnr"])
        dfree(4, 5)
        dve(lambda e: e.tensor_scalar(out=lnr, in0=lnr, scalar1=EPS, scalar2=None, op0=ALU.add), ["lnr"], ["lnr"])
        act(lambda e: e.activation(out=lnr, in_=lnr, func=AF.Sqrt), ["lnr"], ["lnr"])
        dve(lambda e: e.reciprocal(out=lnr, in_=lnr), ["lnr"], ["lnr"])
        for c in range(8):
            ck = f"cxn{c}"
            if c % 2 == 0:
                dve(lambda e, c=c: e.tensor_tensor(out=cx[:, c, :], in0=cx[:, c, :], in1=lnm, op=ALU.subtract), ["cx", "lnm"], [ck])
                dve(lambda e, c=c: e.tensor_tensor(out=cx[:, c, :], in0=cx[:, c, :], in1=lnr, op=ALU.mult), [ck, "lnr"], [ck])
            else:
                S.op("pool", lambda e, c=c: e.tensor_tensor(out=cx[:, c, :], in0=cx[:, c, :], in1=lnm, op=ALU.subtract),
                     reads=["cx", "lnm"], writes=[ck])
                S.op("pool", lambda e, c=c: e.tensor_tensor(out=cx[:, c, :], in0=cx[:, c, :], in1=lnr, op=ALU.mult),
                     reads=[ck, "lnr"], writes=[ck])
            act(lambda e, c=c: e.activation(out=catT[:, 8 + c, :], in_=cx[:, c, :], func=AF.Silu,
                                            bias=prm[:, O_LNB + c:O_LNB + c + 1], scale=prm[:, O_LNG + c:O_LNG + c + 1]),
                [ck, "prm"], ["catT", "cx"])

    def xa_core():
        scale = 512 ** -0.5
        for t in range(4):
            tsl = slice(t * 128, (t + 1) * 128)
            bb = [dbank(), dbank()]
            for hx in range(4):
                b = bb[hx // 2]
                col = (hx % 2) * 256
                for i in range(4):
                    mm(ps[b][:, col:col + 256], xqT[:, 4 * hx + i, tsl], KT[:, 4 * hx + i, :], i == 0, i == 3,
                       ["xqT", "KT"], [f"ps{b}"], i == 3)
            for k in range(2):
                dve(lambda e, k=k: e.tensor_reduce(out=xsm[:, 2 * k:2 * k + 2],
                                                   in_=ps[bb[k]][:].rearrange("p (h m) -> p h m", h=2),
                                                   axis=AX.X, op=ALU.max), [f"ps{bb[k]}"], ["xsm_mx"])
            dve(lambda e: e.tensor_scalar(out=xsm[:, 4:8], in0=xsm[:, 0:4], scalar1=-scale, scalar2=None, op0=ALU.mult),
                ["xsm_mx"], ["xsm_nm"])
            for hx in range(4):
                b = bb[hx // 2]
                col = (hx % 2) * 256
                act(lambda e, b=b, col=col, hx=hx: e.activation(
                    out=ptmp[:, hx, :], in_=ps[b][:, col:col + 256], func=AF.Exp, bias=xsm[:, 4 + hx:5 + hx], scale=scale,
                    accum_out=xsm[:, 8 + hx:9 + hx]), [f"ps{b}", "xsm_nm"], ["ptmp", "xsm_rs"])
            dfree(*bb)
            dve(lambda e: e.reciprocal(out=xsm[:, 12:16], in_=xsm[:, 8:12]), ["xsm_rs"], ["xsm_ri"])
            dve(lambda e: e.tensor_tensor(out=pn, in0=ptmp, in1=xsm[:, 12:16].unsqueeze(2).to_broadcast([128, 4, 256]),
                                          op=ALU.mult), ["ptmp", "xsm_ri"], ["pn"])
            bt = dbank()
            pv = psb[bt][:, 0:1024].rearrange("p (c t) -> p c t", c=8)
            for hx in range(4):
                for mc in range(2):
                    tr(pv[:, hx * 2 + mc, :], pn[:, hx, mc * 128:(mc + 1) * 128], ["pn"], [f"ps{bt}"], hx == 3 and mc == 1)
            act(lambda e, pv=pv, tsl=tsl: e.copy(out=pT[:, :, :, tsl], in_=pv.rearrange("p (h m) t -> p h m t", h=4)),
                [f"ps{bt}"], ["pT"])
            dfree(bt)
        for n in range(16):
            hx = n // 4
            b = gbank()
            for mc in range(2):
                mm(ps[b][:], Vm[:, mc, n * 128:(n + 1) * 128], pT[:, hx, mc, :], mc == 0, mc == 1, ["Vm", "pT"],
                   [f"ps{b}"], mc == 1)
            act(lambda e, b=b, n=n: e.copy(out=xoT[:, n, :], in_=ps[b][:]), [f"ps{b}"], ["xoT"])

    rt8 = sb("rt8", [128, 4, 8], F32)
    pscp = sb("pscp", [128, 64], F32)
    negmb2 = negmb[:, :, :].rearrange("p h e -> p (h e)")

    def dn_group(hg, own, ob):
        h0 = 4 * hg
        sk = lambda n: "sc_" + n
        bc44 = lambda ap: ap.unsqueeze(1).to_broadcast([128, 4, 4])
        act(lambda e: e.activation(out=sc["beta"][:], in_=ba[:, :, h0:h0 + 4], func=AF.Sigmoid), ["ba"], [sk("beta")])
        dve(lambda e: e.tensor_tensor(out=sc["xs"][:], in0=ba[:, :, 8 + h0:12 + h0], in1=bc44(prm[:, O_DTB + h0:O_DTB + h0 + 4]),
                                      op=ALU.add), ["ba", "prm"], [sk("xs")])
        act(lambda e: e.activation(out=sc["axs"][:], in_=sc["xs"][:], func=AF.Abs), [sk("xs")], [sk("axs")])
        act(lambda e: e.activation(out=sc["es"][:], in_=sc["axs"][:], func=AF.Exp, scale=-1.0), [sk("axs")], [sk("es")])
        act(lambda e: e.activation(out=sc["ls"][:], in_=sc["es"][:], func=AF.Ln, bias=prm[:, O_ONE:O_ONE + 1]),
            [sk("es"), "prm"], [sk("ls")])
        dve(lambda e: e.scalar_tensor_tensor(out=sc["xs"][:], in0=sc["xs"][:], scalar=0.0, in1=sc["ls"][:], op0=ALU.max,
                                             op1=ALU.add), [sk("xs"), sk("ls")], [sk("xs")])
        dve(lambda e: e.tensor_tensor(out=sc["graw"][:], in0=sc["xs"][:], in1=bc44(negA[:, h0:h0 + 4]), op=ALU.mult),
            [sk("xs"), "negA"], [sk("graw")])
        dve(lambda e: e.tensor_scalar(out=sc["nbeta"][:], in0=sc["beta"][:], scalar1=-1.0, scalar2=None, op0=ALU.mult),
            [sk("beta")], [sk("nbeta")])
        bs = dbank()
        for t in range(4):
            tsl = slice(t * 128, (t + 1) * 128)
            dve(lambda e: e.tensor_tensor(out=v3(sqt, 8), in0=qkvT[:, 0:8, tsl], in1=qkvT[:, 0:8, tsl], op=ALU.mult),
                ["qkvT"], ["sqt"])
            c0 = t * 16
            for i in range(8):
                mm(ps[bs][:, c0 + i:c0 + i + 1], v3(sqt, 8)[:, i, :], onesb[:, 0:1], True, True, ["sqt", "onesb"], [f"ps{bs}"], False)
            mm(ps[bs][:, c0 + 8:c0 + 12], P(O_U), sc["graw"][:, t, :], True, True, ["prm", sk("graw")], [f"ps{bs}"], False)
            mm(ps[bs][:, c0 + 12:c0 + 16], P(O_ONES), sc["graw"][:, t, :], True, True, ["prm", sk("graw")], [f"ps{bs}"], True)
        act(lambda e: e.copy(out=pscp[:, :], in_=ps[bs][:, 0:64]), [f"ps{bs}"], ["pscp"])
        dfree(bs)
        bs = "cp"
        psv = pscp[:, :].rearrange("p (t n) -> p t n", t=4)
        rsqrt_small(rqk[:, :, :], psv[:, :, 0:8], 1.0, [f"ps{bs}"], ["rqk"], rt8[:, :, :])
        dve(lambda e: e.tensor_scalar(out=sc["rq"][:], in0=rqk[:, :, 0:4], scalar1=128 ** -0.5, scalar2=None, op0=ALU.mult),
            ["rqk"], [sk("rq")])
        act(lambda e: e.copy(out=sc["gcs"][:], in_=psv[:, :, 8:12]), [f"ps{bs}"], [sk("gcs")])
        act(lambda e: e.activation(out=sc["gam"][:], in_=psv[:, :, 8:12], func=AF.Exp), [f"ps{bs}"], [sk("gam")])
        act(lambda e: e.activation(out=sc["eL"][:], in_=psv[:, :, 12:16], func=AF.Exp), [f"ps{bs}"], [sk("eL")])
        dve(lambda e: e.tensor_tensor(out=sc["dl"][:], in0=psv[:, :, 12:16], in1=sc["gcs"][:], op=ALU.subtract),
            [f"ps{bs}", sk("gcs")], [sk("dl")])
        act(lambda e: e.activation(out=sc["edec"][:], in_=sc["dl"][:], func=AF.Exp), [sk("dl")], [sk("edec")])
        dve(lambda e: e.tensor_tensor(out=sc["sB"][:], in0=sc["beta"][:], in1=sc["gam"][:], op=ALU.mult),
            [sk("beta"), sk("gam")], [sk("sB")])
        dve(lambda e: e.tensor_tensor(out=sc["sB"][:], in0=sc["sB"][:], in1=rqk[:, :, 4:8], op=ALU.mult),
            [sk("sB"), "rqk"], [sk("sB")])
        dve(lambda e: e.tensor_tensor(out=sc["sKd"][:], in0=sc["edec"][:], in1=rqk[:, :, 4:8], op=ALU.mult),
            [sk("edec"), "rqk"], [sk("sKd")])
        dve(lambda e: e.tensor_tensor(out=sc["sP1"][:], in0=sc["gam"][:], in1=sc["rq"][:], op=ALU.mult),
            [sk("gam"), sk("rq")], [sk("sP1")])

        def chain(a, b, sfx):
            nh = b - a
            cs = slice(a * 128, b * 128)
            K = lambda n: n + sfx
            V = lambda X: v3(X)[:, a:b, :]
            bch = lambda ap: ap.unsqueeze(2).to_broadcast([128, nh, 128])
            bcm = lambda off: P(off).unsqueeze(1).to_broadcast([128, nh, 128])
            PV = lambda bank: v3(ps[bank][:])[:, a:b, :]
            PVb = lambda bank: v3(psb[bank][:, 0:512])[:, a:b, :]
            for t in range(4):
                tsl = slice(t * 128, (t + 1) * 128)
                g_t = sc["graw"][:, t, a:b]
                S.op("pool", lambda e: e.tensor_tensor(out=V(GU), in0=bcm(O_U), in1=bch(g_t), op=ALU.mult),
                     reads=[sk("graw"), "prm"], writes=[K("GU")])
                bk = dbank()
                for h in range(a, b):
                    tr(v3(psb[bk][:, 0:512])[:, h, :], qkvT[:, 4 + h, tsl], ["qkvT"], [f"ps{bk}"], h == b - 1)
                bv_ = dbank()
                for h in range(a, b):
                    tr(v3(psb[bv_][:, 0:512])[:, h, :], qkvT[:, 8 + h, tsl], ["qkvT"], [f"ps{bv_}"], h == b - 1)
                yield
                dve(lambda e: e.tensor_tensor(out=V(kntok), in0=PVb(bk), in1=bch(rqk[:, t, 4 + a:4 + b]), op=ALU.mult),
                    [f"ps{bk}", "rqk"], [K("kntok")])
                dve(lambda e: e.tensor_tensor(out=V(Bk), in0=PVb(bk), in1=bch(sc["sB"][:, t, a:b]), op=ALU.mult),
                    [f"ps{bk}", sk("sB")], [K("Bk")])
                dve(lambda e: e.tensor_tensor(out=V(kdec), in0=PVb(bk), in1=bch(sc["sKd"][:, t, a:b]), op=ALU.mult),
                    [f"ps{bk}", sk("sKd")], [K("kdec")])
                for h in range(a, b):
                    act(lambda e: e.activation(out=v3(bvv)[:, h, :], in_=v3(psb[bv_][:, 0:512])[:, h, :], func=AF.Copy,
                                               scale=sc["beta"][:, t, h:h + 1]), [f"ps{bv_}", sk("beta")], [K("bvv")])
                dfree(bk, bv_)
                be = dbank()
                mm(ps[be][:, cs], P(O_NEGONES), GU[:, cs], True, False, ["prm", K("GU")], [f"ps{be}"], False)
                mm(ps[be][:, cs], idb[:], negmb2[:, cs], False, True, ["idb", "negmb"], [f"ps{be}"], True)
                yield
                for h in range(a, b):
                    act(lambda e: e.activation(out=v3(Dincl)[:, h, :], in_=ps[be][:, h * 128:(h + 1) * 128], func=AF.Exp,
                                               bias=sc["gcs"][:, t, h:h + 1]), [f"ps{be}", sk("gcs")], [K("Dincl")])
                dfree(be)
                bkt = dbank()
                for h in range(a, b):
                    tr(v3(psb[bkt][:, 0:512])[:, h, :], v3(kntok)[:, h, :], [K("kntok")], [f"ps{bkt}"], h == b - 1)
                yield
                S.op("pool", lambda e: e.tensor_tensor(out=V(Dstr), in0=V(Dincl), in1=bcm(O_STRICT), op=ALU.mult),
                     reads=[K("Dincl"), "prm"], writes=[K("Dstr")])
                act(lambda e: e.copy(out=knT[:, cs], in_=psb[bkt][:, cs]), [f"ps{bkt}"], [K("knT")])
                dfree(bkt)
                yield
                bkk = dbank()
                for h in range(a, b):
                    mm(ps[bkk][:, h * 128:(h + 1) * 128], v3(knT)[:, h, :], v3(knT)[:, h, :], True, True, [K("knT")], [f"ps{bkk}"], h == b - 1)
                if own:
                    bq = dbank()
                    for h in range(a, b):
                        mm(ps[bq][:, h * 128:(h + 1) * 128], qkvT[:, h, tsl], v3(knT)[:, h, :], True, True, ["qkvT", K("knT")],
                           [f"ps{bq}"], h == b - 1)
                yield
                for h in range(a, b):
                    dve(lambda e, h=h: e.scalar_tensor_tensor(
                        out=v3(Nb[0])[:, h, :], in0=ps[bkk][:, h * 128:(h + 1) * 128], scalar=sc["nbeta"][:, t, h:h + 1],
                        in1=v3(Dstr)[:, h, :], op0=ALU.mult, op1=ALU.mult), [f"ps{bkk}", sk("nbeta"), K("Dstr")], [K("Nb0")])
                dfree(bkk)
                if own:
                    for h in range(a, b):
                        dve(lambda e, h=h: e.scalar_tensor_tensor(
                            out=v3(attn)[:, h, :], in0=ps[bq][:, h * 128:(h + 1) * 128], scalar=sc["rq"][:, t, h:h + 1],
                            in1=v3(Dincl)[:, h, :], op0=ALU.mult, op1=ALU.mult), [f"ps{bq}", sk("rq"), K("Dincl")], [K("attn")])
                    dfree(bq)
                yield
                bm = dbank()
                for h in range(a, b):
                    tr(v3(psb[bm][:, 0:512])[:, h, :], v3(Nb[0])[:, h, :], [K("Nb0")], [f"ps{bm}"], h == b - 1)
                if own:
                    bt = dbank()
                    for h in range(a, b):
                        tr(v3(psb[bt][:, 0:512])[:, h, :], v3(attn)[:, h, :], [K("attn")], [f"ps{bt}"], h == b - 1)
                yield
                dve(lambda e: e.tensor_copy(out=Mb[0][:, cs], in_=psb[bm][:, cs]), [f"ps{bm}"], [K("Mb0")])
                dve(lambda e: e.tensor_tensor(out=V(Rb), in0=PVb(bm), in1=bcm(O_ID), op=ALU.add), [f"ps{bm}", "prm"], [K("Rb")])
                dfree(bm)
                if own:
                    act(lambda e: e.copy(out=attnT[:, cs], in_=psb[bt][:, cs]), [f"ps{bt}"], [K("attnT")])
                    dfree(bt)
                yield
                def sq_mms(cur_):
                    bn_ = dbank()
                    for h in range(a, b):
                        mm(ps[bn_][:, h * 128:(h + 1) * 128], v3(Mb[cur_])[:, h, :], v3(Nb[cur_])[:, h, :], True, True,
                           [K(f"Mb{cur_}"), K(f"Nb{cur_}")], [f"ps{bn_}"], h == b - 1)
                    return bn_

                def sqm_mms(cur_):
                    bm_ = dbank()
                    for h in range(a, b):
                        mm(ps[bm_][:, h * 128:(h + 1) * 128], v3(Nb[cur_])[:, h, :], v3(Mb[cur_])[:, h, :], True, True,
                           [K(f"Mb{cur_}"), K(f"Nb{cur_}")], [f"ps{bm_}"], h == b - 1)
                    return bm_

                cur = 0
                bn = sq_mms(0)
                bm2 = sqm_mms(0)
                yield
                act(lambda e: e.copy(out=Nb[1][:, cs], in_=ps[bn][:, cs]), [f"ps{bn}"], [K("Nb1")])
                act(lambda e: e.copy(out=Mb[1][:, cs], in_=ps[bm2][:, cs]), [f"ps{bm2}"], [K("Mb1")])
                dfree(bn, bm2)
                yield
                for lev in range(1, 7):
                    nx = 1 - cur
                    bd = dbank()
                    for h in range(a, b):
                        mm(ps[bd][:, h * 128:(h + 1) * 128], v3(Nb[nx])[:, h, :], v3(Rb)[:, h, :], True, True,
                           [K(f"Nb{nx}"), K("Rb")], [f"ps{bd}"], h == b - 1)
                    if lev < 6:
                        bn = sq_mms(nx)
                        if lev < 5:
                            bm2 = sqm_mms(nx)
                    yield
                    dve(lambda e: e.tensor_tensor(out=Rb[:, cs], in0=ps[bd][:, cs], in1=Rb[:, cs], op=ALU.add),
                        [f"ps{bd}", K("Rb")], [K("Rb")])
                    dfree(bd)
                    if lev < 6:
                        act(lambda e: e.copy(out=Nb[cur][:, cs], in_=ps[bn][:, cs]), [f"ps{bn}"], [K(f"Nb{cur}")])
                        dfree(bn)
                        if lev < 5:
                            act(lambda e: e.copy(out=Mb[cur][:, cs], in_=ps[bm2][:, cs]), [f"ps{bm2}"], [K(f"Mb{cur}")])
                            dfree(bm2)
                    cur = nx
                    yield
                bw = dbank()
                for h in range(a, b):
                    mm(ps[bw][:, h * 128:(h + 1) * 128], v3(Bk)[:, h, :], v3(Rb)[:, h, :], True, True, [K("Bk"), K("Rb")],
                       [f"ps{bw}"], h == b - 1)
                yield
                act(lambda e: e.activation(out=negwT[:, cs], in_=ps[bw][:, cs], func=AF.Copy, scale=-1.0), [f"ps{bw}"], [K("negwT")])
                dfree(bw)
                yield
                bvn = dbank()
                for h in range(a, b):
                    mm(ps[bvn][:, h * 128:(h + 1) * 128], v3(Rb)[:, h, :], v3(bvv)[:, h, :], True, False, [K("Rb"), K("bvv")],
                       [f"ps{bvn}"], False)
                    mm(ps[bvn][:, h * 128:(h + 1) * 128], v3(negwT)[:, h, :], Sbf[:, h0 + h, :], False, True,
                       [K("negwT"), K(f"Sbf{hg}")], [f"ps{bvn}"], h == b - 1)
                if own:
                    bp1 = dbank()
                    for h in range(a, b):
                        mm(ps[bp1][:, h * 128:(h + 1) * 128], qkvT[:, h, tsl], Sbf[:, h0 + h, :], True, True,
                           ["qkvT", K(f"Sbf{hg}")], [f"ps{bp1}"], h == b - 1)
                yield
                act(lambda e: e.copy(out=vnew[:, cs], in_=ps[bvn][:, cs]), [f"ps{bvn}"], [K("vnew")])
                dfree(bvn)
                if own:
                    dve(lambda e: e.tensor_tensor(out=V(tb_), in0=PV(bp1), in1=bch(sc["sP1"][:, t, a:b]), op=ALU.mult),
                        [f"ps{bp1}", sk("sP1")], [K("tb")])
                    dfree(bp1)
                yield
                bs2 = dbank()
                for h in range(a, b):
                    mm(ps[bs2][:, h * 128:(h + 1) * 128], v3(kdec)[:, h, :], v3(vnew)[:, h, :], True, True, [K("kdec"), K("vnew")],
                       [f"ps{bs2}"], h == b - 1)
                if own:
                    bp2 = dbank()
                    for h in range(a, b):
                        mm(ps[bp2][:, h * 128:(h + 1) * 128], v3(attnT)[:, h, :], v3(vnew)[:, h, :], True, True,
                           [K("attnT"), K("vnew")], [f"ps{bp2}"], h == b - 1)
                yield
                for h in range(a, b):
                    dve(lambda e, h=h: e.scalar_tensor_tensor(
                        out=Sst[:, h0 + h, :], in0=Sst[:, h0 + h, :], scalar=sc["eL"][:, t, h:h + 1],
                        in1=ps[bs2][:, h * 128:(h + 1) * 128], op0=ALU.mult, op1=ALU.add),
                        [f"ps{bs2}", sk("eL"), K(f"Sst{hg}")], [K(f"Sst{hg}")])
                dfree(bs2)
                if own:
                    dve(lambda e: e.tensor_tensor(out=yb[:, cs], in0=ps[bp2][:, cs], in1=tb_[:, cs], op=ALU.add),
                        [f"ps{bp2}", K("tb")], [K("yb")])
                    dfree(bp2)
                yield
                act(lambda e: e.copy(out=Sbf[:, h0 + a:h0 + b, :], in_=Sst[:, h0 + a:h0 + b, :]), [K(f"Sst{hg}")], [K(f"Sbf{hg}")])
                if own:
                    dve(lambda e: e.tensor_tensor(out=tb_[:, cs], in0=yb[:, cs], in1=yb[:, cs], op=ALU.mult), [K("yb")], [K("tb")])
                    dve(lambda e: e.tensor_reduce(out=sc["ssy"][:, t, a:b], in_=V(tb_), axis=AX.X, op=ALU.add), [K("tb")], [K("ssy")])
                    yield
                    act(lambda e: e.activation(out=sc["tq"][:, t, a:b], in_=sc["ssy"][:, t, a:b], func=AF.Ln, scale=1.0 / 128,
                                               bias=prm[:, O_EPS:O_EPS + 1]), [K("ssy"), "prm"], [K("tq")])
                    act(lambda e: e.activation(out=sc["ro"][:, t, a:b], in_=sc["tq"][:, t, a:b], func=AF.Exp, scale=-0.5),
                        [K("tq")], [K("ro")])
                    yield
                    dve(lambda e: e.tensor_tensor(out=V(yb), in0=V(yb), in1=bch(sc["ro"][:, t, a:b]), op=ALU.mult),
                        [K("yb"), K("ro")], [K("yb")])
                    dve(lambda e: e.tensor_tensor(out=og[:, cs], in0=yb[:, cs], in1=zs[:, t, (h0 + a) * 128:(h0 + b) * 128], op=ALU.mult),
                        [K("yb"), "zs"], [K("og")])
                    yield
                    bo = dbank()
                    for h in range(a, b):
                        tr(v3(psb[bo][:, 0:512])[:, h, :], v3(og)[:, h, :], [K("og")], [f"ps{bo}"], h == b - 1)
                    yield
                    act(lambda e: e.copy(out=catT[:, h0 + a:h0 + b, tsl], in_=v3(psb[bo][:, 0:512])[:, a:b, :]), [f"ps{bo}"], ["catT"])
                    dfree(bo)
                yield


        if NCHAIN == 4:
            gens = [chain(i, i + 1, "ABCD"[i]) for i in range(4)]
        elif NCHAIN == 2:
            gens = [chain(0, 2, "A"), chain(2, 4, "B")]
        else:
            gens = [chain(0, 4, "A")]
        live = list(gens)
        while live:
            for g in list(live):
                try:
                    next(g)
                except StopIteration:
                    live.remove(g)

    for blk in range(NBLK):
        block(blk)
    run_steps()
    S.final_waits("sp", ["out"] + tapkeys)
    S.emit()
    return nc


def _prm(inp):
    f = np.float32
    p = np.zeros((128, PW), f)
    i = np.arange(128)
    p[:, O_ID:O_ID + 128] = np.eye(128, dtype=f)
    p[:, O_U:O_U + 128] = (i[:, None] <= i[None, :]).astype(f)
    p[:, O_ONES:O_ONES + 128] = 1.0
    p[:, O_NEGONES:O_NEGONES + 128] = -1.0
    p[:, O_STRICT:O_STRICT + 128] = (i[None, :] < i[:, None]).astype(f)
    p[:, O_NEGM:O_NEGM + 128] = np.where(i[None, :] > i[:, None], NEG, 0.0).astype(f)
    col = lambda v: np.ascontiguousarray(np.asarray(v, f).reshape(-1, 128).T)
    p[:, O_GMIX:O_GMIX + 16] = col(inp["norm_mix_g"][0])
    p[:, O_GXA:O_GXA + 16] = col(inp["norm_xa_g"][0])
    p[:, O_GMLP:O_GMLP + 16] = col(inp["norm_mlp_g"][0])
    p[:, O_GMEM:O_GMEM + 16] = col(inp["norm_mem_g"][0])
    p[:, O_DNW:O_DNW + 96] = np.asarray(inp["dn_conv_w"][0], f).reshape(4, 24, 128).transpose(2, 1, 0).reshape(128, 96)
    p[:, O_CFW:O_CFW + 248] = np.asarray(inp["cf_dw_w"][0], f).reshape(31, 8, 128).transpose(2, 1, 0).reshape(128, 248)
    p[:, O_CFB:O_CFB + 8] = col(inp["cf_dw_b"][0])
    p[:, O_LNG:O_LNG + 8] = col(inp["cf_ln_g"][0])
    p[:, O_LNB:O_LNB + 8] = col(inp["cf_ln_b"][0])
    p[:, O_ALOG:O_ALOG + 8] = np.asarray(inp["dn_a_log"][0], f)[None, :]
    p[:, O_DTB:O_DTB + 8] = np.asarray(inp["dn_dt_bias"][0], f)[None, :]
    p[:, O_GN:O_GN + 128] = np.asarray(inp["dn_norm_g"][0], f)[None, :]
    p[:, O_EPS] = EPS
    p[:, O_ONE] = 1.0
    return p


_CACHE = {}


def _get_nc(npre, nown):
    k = (npre, nown)
    if k not in _CACHE:
        _CACHE[k] = build(npre, nown)
    return _CACHE[k]


def make_in_maps(inp, ncores=8, npre=4, nown=4):
    f = np.float32
    x = np.asarray(inp["x"], f)
    mem = np.asarray(inp["mem"], f)
    prm = _prm(inp)
    gfr = np.ascontiguousarray(np.broadcast_to(np.asarray(inp["norm_final_g"], f)[None, :], (128, D)))
    shared = {"prm": prm, "gfr": gfr}
    for k in ("w_in", "w_out", "xa_wq", "xa_wk", "xa_wv", "xa_wo", "mlp_w1", "mlp_w2"):
        shared[k] = np.ascontiguousarray(np.asarray(inp[k], f)[0])
    maps = []
    T2 = nown * TB
    for c in range(ncores):
        b, half = c // 2, c % 2
        own = x[b, half * T2:(half + 1) * T2]
        if npre:
            pre = x[b, 0:npre * TB] if half == 1 else np.zeros((npre * TB, D), f)
            xin = np.concatenate([pre, own], axis=0)
        else:
            xin = own
        m = dict(shared)
        m["xin"] = np.ascontiguousarray(xin)
        m["mem"] = np.ascontiguousarray(mem[b])
        maps.append(m)
    return maps


def kernel(**inputs):
    nc = _get_nc(4, 4)
    maps = make_in_maps(inputs, 8, 4, 4)
    res = run_bass_kernel_spmd(nc, maps, core_ids=list(range(8)))
    out = np.zeros((4, 4096, D), np.float32)
    for c in range(8):
        b, half = c // 2, c % 2
        out[b, half * 2048:(half + 1) * 2048] = res.results[c]["out"]
    return out
```
